# Optimizing a Trainium2 kernel written in Bass

```python
import math
import jax, jax.numpy as jnp
from jax import lax
import numpy as np

D_MODEL = 2048
BATCH = 4
SEQ = 2048
DEPTH = 1

POOL_WIDTH = D_MODEL // 2
POOL_WINDOWS = (2, 4, 8, 16)
POOL_GROUPS = len(POOL_WINDOWS)
POOL_GROUP_DIM = POOL_WIDTH // POOL_GROUPS
SB_HEAD_DIM = 128
SB_HEADS = (D_MODEL // 2) // SB_HEAD_DIM
SB_WIDTH = SB_HEADS * SB_HEAD_DIM
N_BRANCHES = 2
IN_WIDTH = POOL_WIDTH + 3 * SB_WIDTH + N_BRANCHES * D_MODEL
D_FF = 4 * D_MODEL
Q_BLOCK = 128
N_MOD = 6
EPS = 1e-6

kernel_name = "hybrid_pool_stickbreak_gated_block"


def rms_norm(x, w):
    xf = x.astype(jnp.float32)
    y = xf * lax.rsqrt(jnp.mean(jnp.square(xf), axis=-1, keepdims=True) + EPS)
    return (y * w.astype(jnp.float32)).astype(x.dtype)


def multiscale_pool(u, w_pool, pool_scale):
    B, S, _ = u.shape
    uf = u.astype(jnp.float32).reshape(B, S, POOL_GROUPS, POOL_GROUP_DIM)
    cs = jnp.cumsum(uf, axis=1)
    pos = jnp.arange(S, dtype=jnp.int32)
    outs = []
    for g, w in enumerate(POOL_WINDOWS):
        csg = cs[:, :, g]
        lag = jnp.pad(csg, ((0, 0), (w, 0), (0, 0)))[:, :S]
        count = jnp.minimum(pos + 1, w).astype(jnp.float32)[None, :, None]
        outs.append((csg - lag) / count - uf[:, :, g])
    pooled = jnp.stack(outs, axis=2)
    mixed = jnp.einsum('bsgc,gce->bsge', pooled, w_pool.astype(jnp.float32))
    y = mixed.reshape(B, S, POOL_WIDTH) * pool_scale.astype(jnp.float32)
    return y.astype(u.dtype)


def stick_breaking_attention(q, k, v):
    B, S, H, Dh = q.shape
    nb = S // Q_BLOCK
    scale = 1.0 / math.sqrt(Dh)
    kh = k.transpose(0, 2, 1, 3)
    vh = v.transpose(0, 2, 1, 3)
    qb = q.transpose(0, 2, 1, 3).reshape(B, H, nb, Q_BLOCK, Dh).transpose(2, 0, 1, 3, 4)
    starts = jnp.arange(nb, dtype=jnp.int32) * Q_BLOCK
    key_pos = jnp.arange(S, dtype=jnp.int32)

    def block(args):
        q_blk, t0 = args
        z = jnp.einsum('bhqd,bhkd->bhqk', q_blk, kh).astype(jnp.float32) * scale
        q_pos = t0 + jnp.arange(Q_BLOCK, dtype=jnp.int32)
        mask = key_pos[None, :] < q_pos[:, None]
        log_beta = jax.nn.log_sigmoid(z)
        log_1m_beta = log_beta - z
        l = jnp.where(mask, log_1m_beta, 0.0)
        suffix = lax.cumsum(l, axis=3, reverse=True) - l
        a = jnp.where(mask, jnp.exp(log_beta + suffix), 0.0)
        return jnp.einsum('bhqk,bhkd->bhqd', a.astype(vh.dtype), vh)

    out = lax.map(block, (qb, starts))
    return out.transpose(1, 0, 3, 2, 4).reshape(B, S, H * Dh)


def setup_inputs(seed: int = 0) -> dict:
    key = jax.random.key(seed)
    ks = jax.random.split(key, 20)
    f32 = jnp.float32
    L = DEPTH

    def nrm(k, shape, fan_in, gain=1.0):
        return jax.random.normal(k, shape, f32) * (gain * fan_in ** -0.5)

    return {
        "x": jax.random.normal(ks[0], (BATCH, SEQ, D_MODEL), f32),
        "c": jax.random.normal(ks[1], (BATCH, D_MODEL), f32),
        "w_ada": nrm(ks[2], (L, D_MODEL, N_MOD * D_MODEL), D_MODEL, 0.5),
        "b_ada": 0.02 * jax.random.normal(ks[3], (L, N_MOD * D_MODEL), f32),
        "norm1_w": 1.0 + 0.05 * jax.random.normal(ks[4], (L, D_MODEL), f32),
        "w_in": nrm(ks[5], (L, D_MODEL, IN_WIDTH), D_MODEL),
        "q_norm_w": 1.0 + 0.05 * jax.random.normal(ks[6], (L, SB_HEAD_DIM), f32),
        "k_norm_w": 1.0 + 0.05 * jax.random.normal(ks[7], (L, SB_HEAD_DIM), f32),
        "w_pool": nrm(ks[8], (L, POOL_GROUPS, POOL_GROUP_DIM, POOL_GROUP_DIM), POOL_GROUP_DIM),
        "pool_scale": 1.0 + 0.1 * jax.random.normal(ks[9], (L, POOL_WIDTH), f32),
        "w_a_up": nrm(ks[10], (L, POOL_WIDTH, D_MODEL), POOL_WIDTH),
        "w_b_up": nrm(ks[11], (L, SB_WIDTH, D_MODEL), SB_WIDTH),
        "w_o": nrm(ks[12], (L, D_MODEL, D_MODEL), D_MODEL),
        "norm2_w": 1.0 + 0.05 * jax.random.normal(ks[13], (L, D_MODEL), f32),
        "w_ff1": nrm(ks[14], (L, D_MODEL, D_FF), D_MODEL),
        "w_ff2": nrm(ks[15], (L, D_FF, D_MODEL), D_FF),
    }


def reference(x, c, w_ada, b_ada, norm1_w, w_in, q_norm_w, k_norm_w, w_pool, pool_scale,
              w_a_up, w_b_up, w_o, norm2_w, w_ff1, w_ff2):
    B, S, D = x.shape
    split_at = [POOL_WIDTH, POOL_WIDTH + SB_WIDTH, POOL_WIDTH + 2 * SB_WIDTH,
                POOL_WIDTH + 3 * SB_WIDTH, POOL_WIDTH + 3 * SB_WIDTH + D_MODEL]
    for l in range(DEPTH):
        mod = jax.nn.silu(c) @ w_ada[l] + b_ada[l]
        shift1, scale1, gate1, shift2, scale2, gate2 = jnp.split(mod, N_MOD, axis=-1)

        h = rms_norm(x, norm1_w[l]) * (1.0 + scale1[:, None]) + shift1[:, None]
        proj = h @ w_in[l]
        u_pool, q, k, v, g_a, g_b = jnp.split(proj, split_at, axis=-1)

        y_a = multiscale_pool(u_pool, w_pool[l], pool_scale[l]) @ w_a_up[l]

        q = rms_norm(q.reshape(B, S, SB_HEADS, SB_HEAD_DIM), q_norm_w[l])
        k = rms_norm(k.reshape(B, S, SB_HEADS, SB_HEAD_DIM), k_norm_w[l])
        v = v.reshape(B, S, SB_HEADS, SB_HEAD_DIM)
        y_b = stick_breaking_attention(q, k, v) @ w_b_up[l]

        merged = jax.nn.sigmoid(g_a) * y_a + jax.nn.sigmoid(g_b) * y_b
        x = x + gate1[:, None] * (merged @ w_o[l])

        h2 = rms_norm(x, norm2_w[l]) * (1.0 + scale2[:, None]) + shift2[:, None]
        f = jnp.square(jax.nn.relu(h2 @ w_ff1[l])) @ w_ff2[l]
        x = x + gate2[:, None] * f
    return x
```

```python
from contextlib import ExitStack

import numpy as np
import concourse.bass as bass
import concourse.mybir as mybir
from concourse.bass_utils import run_bass_kernel_spmd

F32 = mybir.dt.float32
BF16 = mybir.dt.bfloat16
AF = mybir.ActivationFunctionType
ALU = mybir.AluOpType

import os
STAGE = int(os.environ.get("KSTAGE", "99"))
SUB = int(os.environ.get("KSUB", "99"))
VAR = int(os.environ.get("KVAR", "0"))


class _Stop(Exception):
    pass


ENGS = ["pe", "act", "dve", "pool", "sp"]
NEG = -30000.0
EPS = 1e-6
OWN = [[0, 3, 4, 7, 8, 11, 12, 15], [1, 2, 5, 6, 9, 10, 13, 14]]


class Buf:
    __slots__ = ("name", "w", "r")

    def __init__(self, name):
        self.name = name
        self.w = {}
        self.r = {}

    def set_w(self, ev):
        self.w = {ev[0]: ev[1]}


class Prog:
    def __init__(self, nc):
        self.nc = nc
        self.streams = {e: [] for e in ENGS}
        self.cnt = {}
        self.waited = {}
        self.sems = {}
        self.semnames = []
        for e in ENGS:
            self.new_sem("E_" + e)

    def new_sem(self, key):
        self.semnames.append(key)
        self.cnt[key] = 0
        return key

    def op(self, eng, fn, reads=(), writes=(), sem=None, inc=1, pwrites=()):
        need = {}

        def add(k, v):
            if need.get(k, 0) < v:
                need[k] = v
        for b in reads:
            for k, v in b.w.items():
                add(k, v)
        for b in writes:
            for k, v in b.w.items():
                add(k, v)
            for k, v in b.r.items():
                add(k, v)
        for b in pwrites:
            for k, v in b.r.items():
                add(k, v)
        waits = []
        for k, v in need.items():
            if self.waited.get((eng, k), 0) < v:
                self.waited[(eng, k)] = v
                waits.append((k, v))
        key = sem if sem is not None else "E_" + eng
        self.cnt[key] += inc
        ev = (key, self.cnt[key])
        sems = self.sems

        def run(e, waits=waits, fn=fn, key=key, inc=inc):
            for k, v in waits:
                e.wait_ge(sems[k], v)
            ins = fn(e)
            ins.then_inc(sems[key], inc)
        self.streams[eng].append(run)
        for b in writes:
            b.w = {ev[0]: ev[1]}
            b.r = {}
        for b in pwrites:
            if b.w.get(ev[0], 0) < ev[1]:
                b.w[ev[0]] = ev[1]
        for b in reads:
            if b.r.get(ev[0], 0) < ev[1]:
                b.r[ev[0]] = ev[1]
        return ev

    def barrier(self):
        snap = {("E_" + e): self.cnt["E_" + e] for e in ENGS}
        sems = self.sems
        for e in ENGS:
            waits = []
            for k, v in snap.items():
                if v > 0 and self.waited.get((e, k), 0) < v:
                    self.waited[(e, k)] = v
                    waits.append((k, v))
            if waits:
                def run(h, waits=waits):
                    for k, v in waits:
                        h.wait_ge(sems[k], v)
                self.streams[e].append(run)

    def wait_all(self, eng, keys):
        sems = self.sems
        waits = [(k, self.cnt[k]) for k in keys if self.cnt[k] > 0]

        def run(h, waits=waits):
            for k, v in waits:
                h.wait_ge(sems[k], v)
        self.streams[eng].append(run)

    def build(self):
        nc = self.nc
        with ExitStack() as st:
            for k in self.semnames:
                self.sems[k] = st.enter_context(nc.semaphore(k))
            block = st.enter_context(nc.Block())

            @block.tensor
            def _(h):
                for f in self.streams["pe"]:
                    f(h)

            @block.scalar
            def _(h):
                for f in self.streams["act"]:
                    f(h)

            @block.vector
            def _(h):
                for f in self.streams["dve"]:
                    f(h)

            @block.gpsimd
            def _(h):
                for f in self.streams["pool"]:
                    f(h)

            @block.sync
            def _(h):
                for f in self.streams["sp"]:
                    f(h)


def build_program():
    nc = bass.Bass("TRN2", target_bir_lowering=False)
    P = Prog(nc)

    def din(name, shape):
        return nc.dram_tensor(name, list(shape), F32, kind="ExternalInput").ap()

    xa = din("xa", [2048, 2048])
    xo = din("xo", [1152, 2048])
    csil = din("csil", [128, 16])
    bada = din("bada", [128, 96])
    n1w = din("n1w", [128, 16])
    n2w = din("n2w", [128, 16])
    qkw = din("qkw", [128, 2])
    pscale = din("pscale", [128, 8])
    hv16 = din("hv16", [128, 128])
    invc = din("invc", [128, 512])
    c_ident = din("c_ident", [128, 128])
    c_negtri = din("c_negtri", [128, 128])
    c_ones = din("c_ones", [128, 128])
    c_E = din("c_E", [128, 256])
    c_S = din("c_S", [16, 2048])
    c_negmask = din("c_negmask", [128, 1024])
    w_ada = din("w_ada", [2048, 12288])
    w_in = din("w_in", [2048, 8192])
    w_pool = din("w_pool", [4, 256, 256])
    w_a_up = din("w_a_up", [1024, 2048])
    w_b_up = din("w_b_up", [1024, 2048])
    w_o = din("w_o", [2048, 2048])
    w_ff1 = din("w_ff1", [2048, 8192])
    w_ff2 = din("w_ff2", [8192, 2048])
    out = nc.dram_tensor("out", [1024, 2048], F32, kind="ExternalOutput").ap()

    with ExitStack() as st:
        def sb(name, shape, dt):
            return st.enter_context(nc.sbuf_tensor(name, list(shape), dt))

        ARENA_KB = 159
        arena = sb("arena", [128, ARENA_KB * 512], BF16)

        def av(kb_off, kb_len, dt=BF16):
            a = arena[:, int(kb_off * 512):int((kb_off + kb_len) * 512)]
            return a.bitcast(F32) if dt == F32 else a

        slab_t = [sb("slab0", [128, 16, 512], BF16), sb("slab1", [128, 16, 512], BF16)]
        ident = sb("ident", [128, 128], BF16)
        negtri = sb("negtri", [128, 128], BF16)
        ones_bf = sb("ones_bf", [128, 128], BF16)
        negones = sb("negones", [128, 128], BF16)
        Et = sb("Et", [128, 16, 16], BF16)
        St = sb("St", [16, 16, 128], BF16)
        negmask = sb("negmask", [128, 4, 256], BF16)
        invct = sb("invct", [128, 4, 128], F32)
        hvt = sb("hvt", [128, 8, 16], F32)
        wpool = sb("wpool", [128, 4, 2, 256], BF16)
        csil_t = sb("csil_t", [128, 16], F32)
        sig_t = sb("sig_t", [128, 16], F32)
        scT = sb("scT", [128, 16], BF16)
        bada_t = sb("bada_t", [128, 96], F32)
        modfm = sb("modfm", [128, 96], F32)
        n1w_t = sb("n1w_t", [128, 16], F32)
        n2w_t = sb("n2w_t", [128, 16], F32)
        A1 = sb("A1", [128, 16], F32)
        A2 = sb("A2", [128, 16], F32)
        qkw_t = sb("qkw_t", [128, 2], F32)
        qw_s = sb("qw_s", [128, 1], F32)
        psc_t = sb("psc_t", [128, 8], F32)
        gh_bf = sb("gh_bf", [128, 16], BF16)
        gh32 = sb("gh32", [128, 16], F32)
        gl32 = sb("gl32", [128, 16], F32)
        Dh = [sb("Dh0", [128, 128], BF16), sb("Dh1", [128, 128], BF16)]
        Dl = [sb("Dl0", [128, 128], BF16), sb("Dl1", [128, 128], BF16)]
        DB = [Buf("D0"), Buf("D1")]
        dctr = [0]
        sst = [sb("ss0", [128, 4], F32), sb("ss1", [128, 4], F32)]
        ps = [st.enter_context(nc.psum_tensor("ps%d" % i, [128, 512], F32)) for i in range(8)]
        psB = [Buf("ps%d" % i) for i in range(8)]

        P.new_sem("ld_c_pool")
        P.new_sem("ld_c_sp")
        cbufs = {}
        cq = {}

        def cload(eng, dst, src, name):
            b = Buf(name)
            cbufs[name] = b
            cq[name] = eng
            P.op(eng, lambda e: e.dma_start(out=dst, in_=src), writes=[b], sem="ld_c_" + eng, inc=16)
            return b

        cload("pool", ident[:], c_ident, "ident")
        cload("pool", negtri[:], c_negtri, "negtri")
        cload("pool", ones_bf[:], c_ones, "ones_bf")
        cload("pool", Et[:], c_E.rearrange("p (a b) -> p a b", a=16), "E")
        cload("pool", St[:], c_S.rearrange("p (a b) -> p a b", a=16), "S")
        cload("pool", negmask[:], c_negmask.rearrange("p (a b) -> p a b", a=4), "negmask")
        cload("pool", wpool[:], w_pool.rearrange("g (c p) e -> p g c e", p=128), "wpool")
        cload("sp", invct[:], invc.rearrange("p (a b) -> p a b", a=4), "invc")
        cload("sp", hvt[:], hv16.rearrange("p (a b) -> p a b", a=8), "hv")
        cload("sp", csil_t[:], csil, "csil")
        cload("sp", bada_t[:], bada, "bada")
        cload("sp", n1w_t[:], n1w, "n1w")
        cload("sp", n2w_t[:], n2w, "n2w")
        cload("sp", qkw_t[:], qkw, "qkw")
        cload("sp", psc_t[:], pscale, "psc")
        for name, b in cbufs.items():
            b.set_w(("ld_c_" + cq[name], P.cnt["ld_c_" + cq[name]]))
        CB = cbufs
        CB["negones"] = Buf("negones")

        for i in range(2):
            P.new_sem("ld_s%dlo" % i)
            P.new_sem("ld_s%dhi" % i)
        slabB = [(Buf("s0lo"), Buf("s0hi")), (Buf("s1lo"), Buf("s1hi"))]

        def rows16(w, r0, c0, nk=16):
            return w[r0:r0 + nk * 128, c0:c0 + 512].rearrange("(k p) c -> p k c", p=128)

        sched = []
        for n in list(range(0, 8)):
            sched.append([("full", rows16(w_ada, 0, n * 512))])
        for s in [0, 1, 2, 3]:
            sched.append([("full", rows16(w_in, 0, s * 512))])
        for _half in range(2):
            for s in [4, 5, 6, 7]:
                sched.append([("full", rows16(w_in, 0, s * 512))])
        ADA_LATE = list(range(12, 20)) + [8, 9, 10, 11, 20, 21, 22, 23]
        for n in ADA_LATE:
            sched.append([("full", rows16(w_ada, 0, n * 512))])
        for s in range(4):
            sched.append([("full", rows16(w_in, 0, 4096 + s * 512))])
            sched.append([("lo", rows16(w_a_up, 0, s * 512, 8)), ("hi", rows16(w_b_up, 0, s * 512, 8))])
            sched.append([("full", rows16(w_in, 0, 6144 + s * 512))])
        for s in range(4):
            sched.append([("full", rows16(w_o, 0, s * 512))])
        for q in range(4):
            for s in range(4):
                sched.append([("full", rows16(w_ff1, 0, q * 2048 + s * 512))])
            for s in range(4):
                sched.append([("full", rows16(w_ff2, q * 2048, s * 512))])

        wstate = {"next_issue": 0, "next_use": 0, "slot_of": {}}
        free_slots = [0, 1]

        def issue_next():
            while free_slots and wstate["next_issue"] < len(sched):
                i = wstate["next_issue"]
                slot = free_slots.pop(0)
                wstate["slot_of"][i] = slot
                lo, hi = slabB[slot]
                for (half, src) in sched[i]:
                    if half == "full":
                        P.op("pool", lambda e, slot=slot, src=src: e.dma_start(out=slab_t[slot][:], in_=src),
                             writes=[lo, hi], sem="ld_s%dlo" % slot, inc=16)
                    elif half == "lo":
                        P.op("pool", lambda e, slot=slot, src=src: e.dma_start(out=slab_t[slot][:, 0:8, :], in_=src),
                             writes=[lo], sem="ld_s%dlo" % slot, inc=16)
                    else:
                        P.op("pool", lambda e, slot=slot, src=src: e.dma_start(out=slab_t[slot][:, 8:16, :], in_=src),
                             writes=[hi], sem="ld_s%dhi" % slot, inc=16)
                wstate["next_issue"] += 1

        def wnext():
            i = wstate["next_use"]
            wstate["next_use"] += 1
            if i not in wstate["slot_of"]:
                issue_next()
            slot = wstate["slot_of"][i]
            return slab_t[slot], list(slabB[slot]), slot

        def wrelease(slot):
            free_slots.append(slot)
            issue_next()

        issue_next()

        def mm_group(out_ap, pairs, reads, writes):
            n = len(pairs)

            def fn(e):
                ins = None
                for i, (l, r) in enumerate(pairs):
                    ins = e.matmul(out_ap, lhsT=l, rhs=r, start=(i == 0), stop=(i == n - 1))
                return ins
            return P.op("pe", fn, reads=reads, writes=writes)

        nrm_ctr = [0]

        def norm_block(src_ap, src_bufs, xt_bufs, hns, dst, dstB, A_t, B_t, modB, from_dram):
            i = nrm_ctr[0]
            nrm_ctr[0] += 1
            ss = sst[i % 2]
            ssB = ssBs[i % 2]
            hn_t, hnB = hns[i % len(hns)]
            if from_dram:
                xt_t, xtB, ldk = xt_bufs[i % len(xt_bufs)]
                P.op("sp", lambda e: e.dma_start(out=xt_t, in_=src_ap), writes=[xtB], sem=ldk, inc=16)
                xin, xinB = xt_t, [xtB]
            else:
                xin, xinB = src_ap, src_bufs
            P.op("dve", lambda e: e.memset(ss[:, 0:1], 0.0), writes=[ssB])
            P.op("act", lambda e: e.activation(out=hn_t, in_=xin, func=AF.Square, accum_out=ss[:, 0:1]),
                 reads=xinB, writes=[hnB, ssB])
            P.op("act", lambda e: e.activation(out=ss[:, 1:2], in_=ss[:, 0:1], func=AF.Ln, scale=1.0 / 2048, bias=EPS),
                 reads=[ssB], writes=[ssB])
            P.op("act", lambda e: e.activation(out=ss[:, 2:3], in_=ss[:, 1:2], func=AF.Exp, scale=-0.5), reads=[ssB], writes=[ssB])
            P.op("dve", lambda e: e.tensor_scalar(out=hn_t, in0=xin, scalar1=ss[:, 2:3], scalar2=None, op0=ALU.mult),
                 reads=xinB + [ssB], writes=[hnB])
            b0, b1 = [(4, 5), (6, 7)][i % 2]
            pv = [ps[b0][:].bitcast(BF16), ps[b1][:].bitcast(BF16)]

            def back():
                def tr(e):
                    ins = None
                    for c in range(16):
                        ins = e.transpose(pv[c // 8][:, (c % 8) * 128:(c % 8 + 1) * 128], hn_t[:, c * 128:(c + 1) * 128], ident[:])
                    return ins
                P.op("pe", tr, reads=[hnB, CB["ident"]], writes=[psB[b0], psB[b1]])
                for c in range(16):
                    src = pv[c // 8][:, (c % 8) * 128:(c % 8 + 1) * 128]
                    if c < 8:
                        P.op("act", lambda e, c=c, src=src: e.activation(out=dst[:, c, :], in_=src, func=AF.Identity,
                                                                         bias=B_t[:, c:c + 1], scale=A_t[:, c:c + 1]),
                             reads=[psB[b0], modB], pwrites=[dstB])
                    else:
                        P.op("dve", lambda e, c=c, src=src: e.tensor_scalar(out=dst[:, c, :], in0=src, scalar1=A_t[:, c:c + 1],
                                                                            scalar2=B_t[:, c:c + 1], op0=ALU.mult, op1=ALU.add),
                             reads=[psB[b1], modB], pwrites=[dstB])
            return back

        class NormPipe:
            def __init__(self):
                self.pending = None

            def push(self, args):
                bk = norm_block(*args)
                if self.pending is not None:
                    self.pending()
                self.pending = bk

            def flush(self):
                if self.pending is not None:
                    self.pending()
                self.pending = None

        def norm_seq(arglist):
            npipe = NormPipe()
            for args in arglist:
                npipe.push(args)
            npipe.flush()

        ssBs = [Buf("ss0"), Buf("ss1")]
        qk_ctr = [0]

        def qknorm(psv, psbuf, wcol, wB, dest, destB, tset):
            sq, raw, rt, rinv, tB = tset
            sb_i = 3
            LV = VAR if VAR >= 10 else 99
            if VAR == 15:
                P.op("act", lambda e: e.activation(out=sq, in_=psv, func=AF.Square), reads=[psbuf], writes=[tB["sq"]])
                P.op("dve", lambda e: e.tensor_copy(out=dest, in_=psv), reads=[psbuf], writes=[destB])
                return
            if VAR == 16:
                P.op("dve", lambda e: e.tensor_copy(out=raw, in_=psv), reads=[psbuf], writes=[tB["raw"]])
                P.op("dve", lambda e: e.tensor_copy(out=dest, in_=raw), reads=[tB["raw"]], writes=[destB])
                return
            P.op("dve", lambda e: e.tensor_copy(out=raw, in_=psv), reads=[psbuf], writes=[tB["raw"]])
            P.op("act", lambda e: e.activation(out=sq, in_=raw, func=AF.Square), reads=[tB["raw"]], writes=[tB["sq"]])
            if LV >= 12:
                mm_group(ps[sb_i][:, :], [(ones_bf[:], sq)], reads=[tB["sq"], CB["ones_bf"]], writes=[psB[sb_i]])
                P.op("act", lambda e: e.activation(out=rt, in_=ps[sb_i][:, :], func=AF.Ln, scale=1.0 / 128, bias=EPS),
                     reads=[psB[sb_i]], writes=[tB["rt"]])
            if LV >= 13:
                P.op("act", lambda e: e.activation(out=rinv, in_=rt, func=AF.Exp, scale=-0.5), reads=[tB["rt"]], writes=[tB["rinv"]])
            if LV >= 14:
                P.op("dve", lambda e: e.scalar_tensor_tensor(out=dest, in0=raw, scalar=wcol, in1=rinv, op0=ALU.mult, op1=ALU.mult),
                     reads=[tB["raw"], tB["rinv"], wB], writes=[destB])
            else:
                P.op("dve", lambda e: e.tensor_copy(out=dest, in_=raw), reads=[tB["raw"]], writes=[destB])

        def mk_tset(kb):
            sq = av(kb, 1)
            raw = av(kb + 1, 2, F32)
            rt = av(kb + 3, 2, F32)
            rinv = av(kb + 5, 2, F32)
            return (sq, raw, rt, rinv, {k: Buf(k) for k in ["sq", "raw", "rt", "rinv"]})

        truncated = False
        try:
            modB = Buf("modfm")
            P.op("act", lambda e: e.activation(out=sig_t[:], in_=csil_t[:], func=AF.Sigmoid), reads=[CB["csil"]], writes=[modB])
            scB = Buf("scT")
            P.op("dve", lambda e: e.tensor_tensor(out=scT[:], in0=sig_t[:], in1=csil_t[:], op=ALU.mult),
                 reads=[modB, CB["csil"]], writes=[scB])
            P.op("dve", lambda e: e.tensor_scalar(out=qw_s[:], in0=qkw_t[:, 0:1], scalar1=float(128.0 ** -0.5), scalar2=None,
                                                  op0=ALU.mult), reads=[CB["qkw"]], writes=[CB["qkw"]])
            ada_ctr = [0]

            def ada_cols(n, bank=None):
                j = ada_ctr[0] % 2 if bank is None else bank
                ada_ctr[0] += 1
                sl, sB, slot = wnext()

                def fn(e, sl=sl, j=j):
                    ins = None
                    for c in range(4):
                        for k in range(16):
                            ins = e.matmul(ps[j][:, c:c + 1], lhsT=sl[:, k, c * 128:(c + 1) * 128], rhs=scT[:, k:k + 1],
                                           start=(k == 0), stop=(k == 15))
                    return ins
                P.op("pe", fn, reads=sB + [scB], writes=[psB[j]])
                wrelease(slot)
                P.op("dve", lambda e, n=n, j=j: e.tensor_tensor(out=modfm[:, 4 * n:4 * n + 4], in0=ps[j][:, 0:4],
                                                                in1=bada_t[:, 4 * n:4 * n + 4], op=ALU.add),
                     reads=[psB[j], CB["bada"]], writes=[modB])

            for n in list(range(0, 8)):
                ada_cols(n)
            P.op("dve", lambda e: e.scalar_tensor_tensor(out=A1[:], in0=modfm[:, 16:32], scalar=1.0, in1=n1w_t[:],
                                                         op0=ALU.add, op1=ALU.mult), reads=[modB, CB["n1w"]], writes=[modB])
            B1 = modfm[:, 0:16]
            B2 = modfm[:, 48:64]

            P.barrier()
            if STAGE == 1:
                raise _Stop()
            q_fm = av(0, 16).rearrange("p (h t) -> p h t", h=8)
            mixed = av(16, 16).rearrange("p (c t) -> p c t", c=8)
            h_own = av(32, 36).rearrange("p (k t) -> p k t", k=16)
            xtb = []
            for i in range(2):
                k = P.new_sem("ld_xt%d" % i)
                xtb.append((av(68 + 8 * i, 8, F32), Buf("xt%d" % i), k))
            hns1 = [(av(84, 4), Buf("hn_a")), (av(120, 4), Buf("hn_b"))]
            tsets = [mk_tset(88), mk_tset(95)]
            ucat = av(102, 4.5, F32).rearrange("p (b t) -> p b t", b=8)
            tmpA = av(106.5, 4.5, F32).rearrange("p (b t) -> p b t", b=8)
            tmpB = av(111, 4.5, F32).rearrange("p (b t) -> p b t", b=8)
            pooled = av(115.5, 4).rearrange("p (c t) -> p c t", c=2)
            tmp0 = av(119.5, 0.5, F32)
            qB, mixB = Buf("q"), Buf("mixed")
            hownBs = [Buf("h_own_t0"), Buf("h_own_t1"), Buf("h_own_halo")]
            ucatB, tmpAB, tmpBB, pooledB, tmp0B = Buf("ucat"), Buf("tmpA"), Buf("tmpB"), Buf("pooled"), Buf("tmp0")

            norm_seq([(xo[tb * 128:(tb + 1) * 128, :], None, xtb, hns1,
                       h_own[:, :, tb * 128:(tb + 1) * 128], hownBs[tb // 4], A1, B1, modB, True) for tb in range(9)])

            if STAGE == 2 and SUB == 1:
                P.barrier()
                raise _Stop()
            pbank = [0]

            def nbank(lo=0, hi=3):
                b = lo + pbank[0] % (hi - lo)
                pbank[0] += 1
                return b

            def mk_uset(kb, tag):
                return {"ucat": av(kb, 4.5, F32).rearrange("p (b t) -> p b t", b=8),
                        "tmpA": av(kb + 4.5, 4.5, F32).rearrange("p (b t) -> p b t", b=8),
                        "tmpB": av(kb + 9, 4.5, F32).rearrange("p (b t) -> p b t", b=8),
                        "tmp0": av(kb + 13.5, 0.5, F32),
                        "ucatB": Buf("ucat" + tag), "tmpAB": Buf("tmpA" + tag), "tmpBB": Buf("tmpB" + tag), "tmp0B": Buf("tmp0" + tag)}
            usets = [{"ucat": ucat, "tmpA": tmpA, "tmpB": tmpB, "tmp0": tmp0,
                      "ucatB": ucatB, "tmpAB": tmpAB, "tmpBB": tmpBB, "tmp0B": tmp0B}, mk_uset(124, "2")]
            pooledS = [(pooled, pooledB), (av(138, 4).rearrange("p (c t) -> p c t", c=2), Buf("pooled2"))]
            pending_mix = [None]

            def emit_mix(g):
                pl, plB = pooledS[g % 2]
                for ec in range(2):
                    for th in range(2):
                        b = nbank()
                        mm_group(ps[b][:, :], [(wpool[:, g, k, ec * 128:(ec + 1) * 128], pl[:, k, th * 512:(th + 1) * 512])
                                               for k in range(2)], reads=[CB["wpool"], plB], writes=[psB[b]])
                        P.op("act", lambda e, b=b, g=g, ec=ec, th=th: e.activation(
                            out=mixed[:, 2 * g + ec, th * 512:(th + 1) * 512], in_=ps[b][:, :], func=AF.Identity,
                            scale=psc_t[:, 2 * g + ec:2 * g + ec + 1]), reads=[psB[b], CB["psc"]], writes=[mixB])

            for s in range(2):
                sl, sB, slot = wnext()
                for uc4 in range(4):
                    uc = s * 4 + uc4
                    g = uc // 2
                    cc = uc % 2
                    w = 2 ** (g + 1)
                    U = usets[uc % 2]
                    uct, uctB = U["ucat"], U["ucatB"]
                    pl, plB = pooledS[g % 2]
                    bm = []
                    for th in range(2):
                        b = nbank()
                        mm_group(ps[b][:, :], [(sl[:, k, uc4 * 128:(uc4 + 1) * 128], h_own[:, k, th * 512:(th + 1) * 512])
                                               for k in range(16)], reads=sB + [hownBs[th]], writes=[psB[b]])
                        bm.append(b)
                    bh = nbank()
                    mm_group(ps[bh][:, 0:128], [(sl[:, k, uc4 * 128:(uc4 + 1) * 128], h_own[:, k, 1024:1152]) for k in range(16)],
                             reads=sB + [hownBs[2]], writes=[psB[bh]])
                    P.op("act", lambda e, b=bm[0], uct=uct: e.activation(out=uct[:, 0:4, 16:144],
                                                                         in_=ps[b][:, :].rearrange("p (b t) -> p b t", b=4), func=AF.Identity),
                         reads=[psB[bm[0]]], pwrites=[uctB])
                    P.op("act", lambda e, b=bm[1], uct=uct: e.activation(out=uct[:, 4:8, 16:144],
                                                                         in_=ps[b][:, :].rearrange("p (b t) -> p b t", b=4), func=AF.Identity),
                         reads=[psB[bm[1]]], pwrites=[uctB])
                    P.op("dve", lambda e, b=bh, uct=uct: e.tensor_tensor(out=uct[:, :, 0:16], in0=ps[b][:, 0:128].rearrange("p (b t) -> p b t", b=8),
                                                                         in1=hvt[:], op=ALU.mult), reads=[psB[bh], CB["hv"]], pwrites=[uctB])
                    if pending_mix[0] is not None:
                        emit_mix(pending_mix[0])
                        pending_mix[0] = None
                    tA, tAB, tB_, tBB = U["tmpA"], U["tmpAB"], U["tmpB"], U["tmpBB"]
                    P.op("dve", lambda e, tA=tA, uct=uct: e.tensor_tensor(out=tA[:, :, 1:144], in0=uct[:, :, 1:144], in1=uct[:, :, 0:143], op=ALU.add),
                         reads=[uctB], writes=[tAB])
                    cur, curB, oth, othB = tA, tAB, tB_, tBB
                    sh = 2
                    while sh < w:
                        lo_ = 2 * sh - 1
                        P.op("dve", lambda e, cur=cur, oth=oth, lo_=lo_, sh=sh: e.tensor_tensor(
                            out=oth[:, :, lo_:144], in0=cur[:, :, lo_:144], in1=cur[:, :, lo_ - sh:144 - sh], op=ALU.add),
                            reads=[curB], writes=[othB])
                        cur, curB, oth, othB = oth, othB, cur, curB
                        sh *= 2
                    t0_, t0B_ = U["tmp0"], U["tmp0B"]
                    P.op("dve", lambda e, cur=cur, w=w, cc=cc, pl=pl, uct=uct: e.scalar_tensor_tensor(
                        out=pl[:, cc, 128:1024].rearrange("p (b t) -> p b t", b=7), in0=cur[:, 1:8, 16:144], scalar=1.0 / w,
                        in1=uct[:, 1:8, 16:144], op0=ALU.mult, op1=ALU.subtract), reads=[curB, uctB], pwrites=[plB])
                    P.op("dve", lambda e, cur=cur, g=g, t0_=t0_: e.tensor_tensor(out=t0_, in0=cur[:, 0, 16:144], in1=invct[:, g, :], op=ALU.mult),
                         reads=[curB, CB["invc"]], writes=[t0B_])
                    P.op("dve", lambda e, cc=cc, pl=pl, t0_=t0_, uct=uct: e.tensor_tensor(out=pl[:, cc, 0:128], in0=t0_, in1=uct[:, 0, 16:144],
                                                                                          op=ALU.subtract),
                         reads=[t0B_, uctB], pwrites=[plB])
                    if cc == 1:
                        pending_mix[0] = g
                wrelease(slot)
            if pending_mix[0] is not None:
                emit_mix(pending_mix[0])
                pending_mix[0] = None
            if STAGE == 2 and SUB == 2:
                P.barrier()
                raise _Stop()
            qc = [0]
            pend = None
            for s in range(2):
                sl, sB, slot = wnext()
                for h4 in range(4):
                    h = s * 4 + h4
                    for th in range(2):
                        b = nbank()
                        mm_group(ps[b][:, :], [(sl[:, k, h4 * 128:(h4 + 1) * 128], h_own[:, k, th * 512:(th + 1) * 512])
                                               for k in range(16)], reads=sB + [hownBs[th]], writes=[psB[b]])
                        if pend is not None:
                            qknorm(*pend)
                        pend = (ps[b][:, :], psB[b], qw_s[:, 0:1], CB["qkw"], q_fm[:, h, th * 512:(th + 1) * 512], qB,
                                tsets[qc[0] % 2])
                        qc[0] += 1
                wrelease(slot)
            qknorm(*pend)

            P.barrier()
            if STAGE == 2:
                raise _Stop()
            k_fm = av(32, 32).rearrange("p (h t) -> p h t", h=8)
            v_tm = av(64, 32).rearrange("p (b c) -> p b c", b=16)
            h_half = av(96, 32).rearrange("p (k t) -> p k t", k=16)
            kx = P.new_sem("ld_xta")
            xta = [(av(128, 8, F32), Buf("xta"), kx)]
            hns2 = [(av(136, 4), Buf("hn2a")), (av(140, 4), Buf("hn2b"))]
            tsets_a = [mk_tset(144), mk_tset(151)]
            kc = [0]
            kB, vB = Buf("k"), Buf("v")
            hhBs = [Buf("h_half_%d" % i) for i in range(8)]

            def nargs(half, tb):
                return (xa[(half * 8 + tb) * 128:(half * 8 + tb + 1) * 128, :], None, xta, hns2,
                        h_half[:, :, tb * 128:(tb + 1) * 128], hhBs[tb], A1, B1, modB, True)
            npipe = NormPipe()
            for tb in range(8):
                npipe.push(nargs(0, tb))
            npipe.flush()
            for half in range(2):
                pend = None
                for s in range(2):
                    sl, sB, slot = wnext()
                    for th in range(2):
                        for h4 in range(4):
                            h = s * 4 + h4
                            b = nbank()
                            mm_group(ps[b][:, :], [(sl[:, k, h4 * 128:(h4 + 1) * 128], h_half[:, k, th * 512:(th + 1) * 512])
                                                   for k in range(16)], reads=sB + hhBs[4 * th:4 * th + 4], writes=[psB[b]])
                            t0 = half * 1024 + th * 512
                            if pend is not None:
                                qknorm(*pend)
                            pend = (ps[b][:, :], psB[b], qkw_t[:, 1:2], CB["qkw"], k_fm[:, h, t0:t0 + 512], kB, tsets_a[kc[0] % 2])
                            kc[0] += 1
                    wrelease(slot)
                qknorm(*pend)
                for s in range(2):
                    sl, sB, slot = wnext()
                    for tb in range(8):
                        b = nbank()
                        mm_group(ps[b][:, :], [(h_half[:, k, tb * 128:(tb + 1) * 128], sl[:, k, :]) for k in range(16)],
                                 reads=sB + [hhBs[tb]], writes=[psB[b]])
                        gtb = half * 8 + tb
                        if tb % 2 == 0:
                            P.op("act", lambda e, b=b, gtb=gtb, s=s: e.activation(out=v_tm[:, gtb, s * 512:(s + 1) * 512], in_=ps[b][:, :],
                                                                                 func=AF.Identity), reads=[psB[b]], pwrites=[vB])
                        else:
                            P.op("dve", lambda e, b=b, gtb=gtb, s=s: e.tensor_copy(out=v_tm[:, gtb, s * 512:(s + 1) * 512], in_=ps[b][:, :]),
                                 reads=[psB[b]], pwrites=[vB])
                        if half == 0 and s == 1:
                            npipe.push(nargs(1, tb))
                    wrelease(slot)
                if half == 0:
                    npipe.flush()

            P.barrier()
            if STAGE == 3:
                raise _Stop()
            o_fm = av(96, 16).rearrange("p (h t) -> p h t", h=8)
            oB = Buf("o_fm")

            def ring(kb0, kb_each, n, dt, name):
                return [(av(kb0 + j * kb_each, kb_each, dt), Buf("%s%d" % (name, j))) for j in range(n)]
            e_r = ring(112, 2, 3, F32, "e")
            l_r = ring(118, 1, 3, BF16, "l")
            a_r = ring(121, 1, 3, BF16, "a")
            R32 = ring(124, 2, 2, F32, "R32")
            Rb = ring(128, 1, 2, BF16, "Rb")
            zeros_bf = av(130, 1)
            zerosB = Buf("zeros")
            P.op("dve", lambda e: e.memset(zeros_bf, 0.0), writes=[zerosB])
            P.op("dve", lambda e: e.tensor_scalar(out=negones[:], in0=ones_bf[:], scalar1=-1.0, scalar2=None, op0=ALU.mult),
                 reads=[CB["ones_bf"]], writes=[CB["negones"]])
            steps = []
            for hh in range(4):
                for i in range(4):
                    n = 4 * i + 4
                    for j, p in enumerate(range(n - 1, -1, -1)):
                        steps.append({"hs": (hh, hh + 4), "i": i, "n": n, "p": p, "j": j})
            zr = [kB, qB, CB["ident"], CB["negmask"]]

            def zmm(e, o_, h, i, p, first, last):
                msk = p >= 4 * i
                ins = e.matmul(o_, lhsT=k_fm[:, h, p * 128:(p + 1) * 128], rhs=q_fm[:, h, i * 256:(i + 1) * 256],
                               start=first, stop=(last and not msk))
                if msk:
                    ins = e.matmul(o_, lhsT=ident[:], rhs=negmask[:, p - 4 * i, :], start=False, stop=last)
                return ins

            def st1(m):
                s_ = steps[m]
                i, p = s_["i"], s_["p"]
                zb = m % 2

                def fn(e):
                    ins = None
                    for sl_, h in enumerate(s_["hs"]):
                        ins = zmm(e, ps[zb][:, sl_ * 256:(sl_ + 1) * 256], h, i, p, True, True)
                    return ins
                P.op("pe", fn, reads=zr, writes=[psB[zb]])

            def st2(m):
                zb = m % 2
                ee, eeB = e_r[m % 3]
                P.op("act", lambda e: e.activation(out=ee, in_=ps[zb][:, :], func=AF.Exp), reads=[psB[zb]], writes=[eeB])

            def st3(m):
                ee, eeB = e_r[m % 3]
                l_, lB_ = l_r[m % 3]
                P.op("act", lambda e: e.activation(out=l_, in_=ee, func=AF.Ln, bias=1.0), reads=[eeB], writes=[lB_])

            def st4(m):
                s_ = steps[m]
                i, j, p = s_["i"], s_["j"], s_["p"]
                l_, lB_ = l_r[m % 3]
                sbk = 2 + m % 3
                r_old, r_oldB = R32[j % 2]
                r_new, r_newB = R32[(j + 1) % 2]
                rb_old, rb_oldB = Rb[j % 2]
                rb_new, rb_newB = Rb[(j + 1) % 2]

                def fn(e):
                    ins = None
                    for sl_, h in enumerate(s_["hs"]):
                        o_ = ps[sbk][:, sl_ * 256:(sl_ + 1) * 256]
                        zmm(e, o_, h, i, p, True, False)
                        ins = e.matmul(o_, lhsT=negtri[:], rhs=l_[:, sl_ * 256:(sl_ + 1) * 256], start=False, stop=(j == 0))
                        if j > 0:
                            ins = e.matmul(o_, lhsT=negones[:], rhs=rb_old[:, sl_ * 256:(sl_ + 1) * 256], start=False, stop=True)
                    return ins
                rd = zr + [lB_, CB["negtri"]] + ([rb_oldB, CB["negones"]] if j > 0 else [])
                P.op("pe", fn, reads=rd, writes=[psB[sbk]])
                if p > 0:
                    if j == 0:
                        P.op("dve", lambda e: e.tensor_tensor(out=r_new, in0=l_, in1=zeros_bf, op=ALU.add),
                             reads=[lB_, zerosB], writes=[r_newB])
                        P.op("pool", lambda e: e.tensor_tensor(out=rb_new, in0=l_, in1=zeros_bf, op=ALU.add),
                             reads=[lB_, zerosB], writes=[rb_newB])
                    else:
                        P.op("dve", lambda e: e.tensor_tensor(out=r_new, in0=r_old, in1=l_, op=ALU.add),
                             reads=[r_oldB, lB_], writes=[r_newB])
                        P.op("pool", lambda e: e.tensor_tensor(out=rb_new, in0=r_old, in1=l_, op=ALU.add),
                             reads=[r_oldB, lB_], writes=[rb_newB])

            def st5(m):
                a_, aB_ = a_r[m % 3]
                sbk = 2 + m % 3
                P.op("act", lambda e: e.activation(out=a_, in_=ps[sbk][:, :], func=AF.Exp), reads=[psB[sbk]], writes=[aB_])

            def st6(m):
                s_ = steps[m]
                i, p, n = s_["i"], s_["p"], s_["n"]
                a_, aB_ = a_r[m % 3]

                def fn(e):
                    ins = None
                    for sl_, h in enumerate(s_["hs"]):
                        ins = e.matmul(ps[6 + sl_][:, 0:256], lhsT=v_tm[:, p, h * 128:(h + 1) * 128],
                                       rhs=a_[:, sl_ * 256:(sl_ + 1) * 256], start=(p == n - 1), stop=(p == 0))
                    return ins
                P.op("pe", fn, reads=[vB, aB_], writes=[psB[6], psB[7]])
                if p == 0:
                    for sl_, h in enumerate(s_["hs"]):
                        P.op("dve", lambda e, sl_=sl_, h=h: e.tensor_copy(out=o_fm[:, h, i * 256:(i + 1) * 256],
                                                                          in_=ps[6 + sl_][:, 0:256]),
                             reads=[psB[6 + sl_]], pwrites=[oB])

            stages = [st1, st2, st3, st4, st5, st6]
            NS = len(steps)
            ada_late = list(ADA_LATE)
            for k in range(NS + len(stages) - 1):
                for d_, fn_ in enumerate(stages):
                    if 0 <= k - d_ < NS:
                        fn_(k - d_)
                if k % 10 == 4 and ada_late:
                    ada_cols(ada_late.pop(0), bank=5)
            while ada_late:
                ada_cols(ada_late.pop(0), bank=5)

            P.barrier()
            if STAGE == 4:
                raise _Stop()
            h_own3 = av(32, 36).rearrange("p (k t) -> p k t", k=16)
            hown3Bs = [Buf("h_own3_t0"), Buf("h_own3_t1")]
            xtb3 = []
            for i in range(2):
                k = P.new_sem("ld_xu%d" % i)
                xtb3.append((av(68 + 8 * i, 8, F32), Buf("xu%d" % i), k))
            hns3 = [(av(84, 4), Buf("hn3a")), (av(92, 4), Buf("hn3b"))]
            merged = av(112, 32).rearrange("p (c t) -> p c t", c=16)
            mergedB = Buf("merged")
            sg = av(144, 8).rearrange("p (c t) -> p c t", c=4)
            sgB = Buf("sg")
            t1 = av(0, 16, F32).rearrange("p (c t) -> p c t", c=4)
            t1B = Buf("t1")
            t2s = [av(88, 2, F32), av(90, 2, F32)]
            t2B = [Buf("t2a"), Buf("t2b")]
            norm_seq([(xo[tb * 128:(tb + 1) * 128, :], None, xtb3, hns3,
                       h_own3[:, :, tb * 128:(tb + 1) * 128], hown3Bs[tb // 4], A1, B1, modB, True) for tb in range(8)])
            tc = [0]
            for s in range(4):
                slga, sBga, slotga = wnext()
                for th in range(2):
                    for c in range(4):
                        b = nbank()
                        mm_group(ps[b][:, :], [(slga[:, k, c * 128:(c + 1) * 128], h_own3[:, k, th * 512:(th + 1) * 512])
                                               for k in range(16)], reads=sBga + [hown3Bs[th]], writes=[psB[b]])
                        P.op("act", lambda e, b=b, c=c, th=th: e.activation(out=sg[:, c, th * 512:(th + 1) * 512], in_=ps[b][:, :],
                                                                           func=AF.Sigmoid), reads=[psB[b]], writes=[sgB])
                wrelease(slotga)
                slab_, sBab, slotab = wnext()
                for c in range(4):
                    for th in range(2):
                        b = nbank()
                        mm_group(ps[b][:, :], [(slab_[:, k, c * 128:(c + 1) * 128], mixed[:, k, th * 512:(th + 1) * 512])
                                               for k in range(8)], reads=[sBab[0], mixB], writes=[psB[b]])
                        P.op("dve", lambda e, b=b, c=c, th=th: e.tensor_tensor(out=t1[:, c, th * 512:(th + 1) * 512], in0=ps[b][:, :],
                                                                              in1=sg[:, c, th * 512:(th + 1) * 512], op=ALU.mult),
                             reads=[psB[b], sgB], writes=[t1B])
                slgb, sBgb, slotgb = wnext()
                for c in range(4):
                    for th in range(2):
                        b = nbank()
                        mm_group(ps[b][:, :], [(slgb[:, k, c * 128:(c + 1) * 128], h_own3[:, k, th * 512:(th + 1) * 512])
                                               for k in range(16)], reads=sBgb + [hown3Bs[th]], writes=[psB[b]])
                        P.op("act", lambda e, b=b, c=c, th=th: e.activation(out=sg[:, c, th * 512:(th + 1) * 512], in_=ps[b][:, :],
                                                                           func=AF.Sigmoid), reads=[psB[b]], writes=[sgB])
                wrelease(slotgb)
                for c in range(4):
                    for th in range(2):
                        b = nbank()
                        mm_group(ps[b][:, :], [(slab_[:, 8 + k, c * 128:(c + 1) * 128], o_fm[:, k, th * 512:(th + 1) * 512])
                                               for k in range(8)], reads=[sBab[1], oB], writes=[psB[b]])
                        ti = tc[0] % 2
                        tc[0] += 1
                        P.op("dve", lambda e, b=b, c=c, th=th, ti=ti: e.tensor_tensor(out=t2s[ti], in0=ps[b][:, :],
                                                                                     in1=sg[:, c, th * 512:(th + 1) * 512], op=ALU.mult),
                             reads=[psB[b], sgB], writes=[t2B[ti]])
                        P.op("pool", lambda e, c=c, th=th, ti=ti, s=s: e.tensor_tensor(out=merged[:, 4 * s + c, th * 512:(th + 1) * 512],
                                                                                      in0=t1[:, c, th * 512:(th + 1) * 512], in1=t2s[ti],
                                                                                      op=ALU.add), reads=[t1B, t2B[ti]], writes=[mergedB])
                wrelease(slotab)

            P.barrier()
            if STAGE == 5:
                raise _Stop()
            gate_bc = [av(96, 8, F32), av(104, 8, F32)]
            gateB = [Buf("g1"), Buf("g2")]
            A2B = Buf("A2")
            P.op("dve", lambda e: e.scalar_tensor_tensor(out=A2[:], in0=modfm[:, 64:80], scalar=1.0, in1=n2w_t[:],
                                                         op0=ALU.add, op1=ALU.mult), reads=[modB, CB["n2w"]], writes=[A2B])
            gB = Buf("gtmp")
            for gi, c0 in enumerate([32, 80]):
                P.op("dve", lambda e, c0=c0: e.tensor_copy(out=gh_bf[:], in_=modfm[:, c0:c0 + 16]), reads=[modB], writes=[gB])
                P.op("dve", lambda e: e.tensor_copy(out=gh32[:], in_=gh_bf[:]), reads=[gB], writes=[gB])
                P.op("dve", lambda e, c0=c0: e.tensor_tensor(out=gl32[:], in0=modfm[:, c0:c0 + 16], in1=gh32[:], op=ALU.subtract),
                     reads=[modB, gB], writes=[gB])
                for c4 in range(4):
                    b = nbank()
                    for cc in range(4):
                        c = c4 * 4 + cc
                        di = dctr[0] % 2
                        dctr[0] += 1
                        P.op("dve", lambda e, c=c, di=di: e.tensor_scalar(out=Dh[di][:], in0=ident[:], scalar1=gh32[:, c:c + 1], scalar2=None,
                                                                         op0=ALU.mult), reads=[gB, CB["ident"]], writes=[DB[di]])
                        P.op("dve", lambda e, c=c, di=di: e.tensor_scalar(out=Dl[di][:], in0=ident[:], scalar1=gl32[:, c:c + 1], scalar2=None,
                                                                         op0=ALU.mult), reads=[gB, CB["ident"]], writes=[DB[di]])
                        mm_group(ps[b][:, cc * 128:(cc + 1) * 128], [(ones_bf[:], Dh[di][:]), (ones_bf[:], Dl[di][:])],
                                 reads=[DB[di], CB["ones_bf"]], writes=[psB[b]])
                    P.op("act", lambda e, b=b, gi=gi, c4=c4: e.activation(out=gate_bc[gi][:, c4 * 512:(c4 + 1) * 512], in_=ps[b][:, :],
                                                                         func=AF.Identity), reads=[psB[b]], writes=[gateB[gi]])

            x1 = av(32, 64, F32).rearrange("p (b d) -> p b d", b=8)
            x1B = [Buf("x1_%d" % i) for i in range(8)]
            h2_fm = av(0, 32).rearrange("p (k t) -> p k t", k=16)
            h2Bs = [Buf("h2_t0"), Buf("h2_t1")]
            t4 = [av(144, 2, F32), av(146, 2, F32)]
            t4B = [Buf("t4a"), Buf("t4b")]
            hns4 = [(av(148, 4), Buf("hn4a")), (av(152, 4), Buf("hn4b"))]
            P.new_sem("ld_x1")
            for tb in range(8):
                P.op("sp", lambda e, tb=tb: e.dma_start(out=x1[:, tb, :], in_=xo[tb * 128:(tb + 1) * 128, :]),
                     writes=[x1B[tb]], sem="ld_x1", inc=16)
            fin = ("ld_x1", P.cnt["ld_x1"])
            for tb in range(8):
                x1B[tb].set_w(fin)

            def resid_epilogue(b, gi, s, tb, tbuf, tbufB):
                P.op("dve", lambda e: e.tensor_tensor(out=tbuf, in0=ps[b][:, :], in1=gate_bc[gi][:, s * 512:(s + 1) * 512], op=ALU.mult),
                     reads=[psB[b], gateB[gi]], writes=[tbufB])
                P.op("pool", lambda e: e.tensor_tensor(out=x1[:, tb, s * 512:(s + 1) * 512], in0=x1[:, tb, s * 512:(s + 1) * 512],
                                                       in1=tbuf, op=ALU.add), reads=[tbufB, x1B[tb]], writes=[x1B[tb]])

            np4 = NormPipe()
            for s in range(4):
                sl, sB, slot = wnext()
                for tb in range(8):
                    b = nbank()
                    mm_group(ps[b][:, :], [(merged[:, k, tb * 128:(tb + 1) * 128], sl[:, k, :]) for k in range(16)],
                             reads=sB + [mergedB], writes=[psB[b]])
                    ti = tc[0] % 2
                    tc[0] += 1
                    resid_epilogue(b, 0, s, tb, t4[ti], t4B[ti])
                    if s == 3:
                        np4.push((x1[:, tb, :], [x1B[tb]], None, hns4, h2_fm[:, :, tb * 128:(tb + 1) * 128], h2Bs[tb // 4],
                                  A2, B2, modB, False))
                wrelease(slot)
            np4.flush()

            a_q = av(112, 32).rearrange("p (f t) -> p f t", f=16)
            aqB = Buf("a_q")
            r5 = [av(96, 2, F32), av(98, 2, F32)]
            r5B = [Buf("r5a"), Buf("r5b")]
            t5 = [av(100, 2, F32), av(102, 2, F32)]
            t5B = [Buf("t5a"), Buf("t5b")]
            for q in range(4):
                for s in range(4):
                    sl, sB, slot = wnext()
                    for th in range(2):
                        for c in range(4):
                            b = nbank()
                            mm_group(ps[b][:, :], [(sl[:, k, c * 128:(c + 1) * 128], h2_fm[:, k, th * 512:(th + 1) * 512])
                                                   for k in range(16)], reads=sB + [h2Bs[th]], writes=[psB[b]])
                            ri = tc[0] % 2
                            tc[0] += 1
                            P.op("act", lambda e, b=b, ri=ri: e.activation(out=r5[ri], in_=ps[b][:, :], func=AF.Relu),
                                 reads=[psB[b]], writes=[r5B[ri]])
                            P.op("pool", lambda e, ri=ri, s=s, c=c, th=th: e.tensor_tensor(out=a_q[:, 4 * s + c, th * 512:(th + 1) * 512],
                                                                                          in0=r5[ri], in1=r5[ri], op=ALU.mult),
                                 reads=[r5B[ri]], writes=[aqB])
                    wrelease(slot)
                for s in range(4):
                    sl, sB, slot = wnext()
                    for tb in range(8):
                        b = nbank()
                        mm_group(ps[b][:, :], [(a_q[:, k, tb * 128:(tb + 1) * 128], sl[:, k, :]) for k in range(16)],
                                 reads=sB + [aqB], writes=[psB[b]])
                        ti = tc[0] % 2
                        tc[0] += 1
                        resid_epilogue(b, 1, s, tb, t5[ti], t5B[ti])
                    wrelease(slot)
        except _Stop:
            truncated = True
        P.new_sem("st_out")
        if truncated:
            x1 = av(32, 64, F32).rearrange("p (b d) -> p b d", b=8)
            x1B = [Buf("x1t_%d" % i) for i in range(8)]
            P.barrier()
            for tb in range(8):
                P.op("dve", lambda e, tb=tb: e.memset(x1[:, tb, :], 0.0), writes=[x1B[tb]])
        for tb in range(8):
            P.op("sp", lambda e, tb=tb: e.dma_start(out=out[tb * 128:(tb + 1) * 128, :], in_=x1[:, tb, :]),
                 reads=[x1B[tb]], sem="st_out", inc=16)
        P.wait_all("sp", ["st_out"] + [k for k in P.semnames if k.startswith("ld_")])
        if not truncated:
            assert wstate["next_use"] == len(sched), (wstate["next_use"], len(sched))
        print("op counts", {k: v for k, v in P.cnt.items()})
        P.build()
    return nc


def _consts():
    ident = np.eye(128, dtype=np.float32)
    j = np.arange(128)
    negtri = -(j[:, None] >= j[None, :]).astype(np.float32)
    ones = np.ones((128, 128), np.float32)
    E = np.zeros((128, 16, 16), np.float32)
    for p in range(16):
        E[:, p, p] = 1.0
    S = np.zeros((16, 16, 128), np.float32)
    for p in range(16):
        S[p + 1:, p, :] = -1.0
    return ident, negtri, ones, E.reshape(128, 256), S.reshape(16, 2048)


def _negmask(half):
    goff = [0, 3] if half == 0 else [1, 2]
    m = np.zeros((128, 4, 256), np.float32)
    s = np.arange(128)[:, None]
    t = np.arange(128)[None, :]
    tri = np.where(s < t, 0.0, NEG).astype(np.float32)
    for o in range(4):
        for qh in range(2):
            sl = slice(qh * 128, (qh + 1) * 128)
            if o < goff[qh]:
                m[:, o, sl] = 0.0
            elif o == goff[qh]:
                m[:, o, sl] = tri
            else:
                m[:, o, sl] = NEG
    return m.reshape(128, 1024)


_NC_CACHE = {}


def kernel(x, c, w_ada, b_ada, norm1_w, w_in, q_norm_w, k_norm_w, w_pool, pool_scale,
           w_a_up, w_b_up, w_o, norm2_w, w_ff1, w_ff2):
    f = lambda a: np.ascontiguousarray(np.asarray(a, dtype=np.float32))
    x = f(x); c = f(c)
    if "nc" not in _NC_CACHE:
        _NC_CACHE["nc"] = build_program()
    nc = _NC_CACHE["nc"]
    ident, negtri, ones, E, S = _consts()
    bada_fm = f(np.asarray(b_ada)[0].reshape(96, 128).T)
    n1 = f(np.asarray(norm1_w)[0].reshape(16, 128).T)
    n2 = f(np.asarray(norm2_w)[0].reshape(16, 128).T)
    qkw = f(np.stack([np.asarray(q_norm_w)[0], np.asarray(k_norm_w)[0]], axis=1))
    psc = f(np.asarray(pool_scale)[0].reshape(8, 128).T)
    shared = {
        "bada": bada_fm, "n1w": n1, "n2w": n2, "qkw": qkw, "pscale": psc,
        "c_ident": ident, "c_negtri": negtri, "c_ones": ones, "c_E": E, "c_S": S,
        "w_ada": f(np.asarray(w_ada)[0]), "w_in": f(np.asarray(w_in)[0]), "w_pool": f(np.asarray(w_pool)[0]),
        "w_a_up": f(np.asarray(w_a_up)[0]), "w_b_up": f(np.asarray(w_b_up)[0]), "w_o": f(np.asarray(w_o)[0]),
        "w_ff1": f(np.asarray(w_ff1)[0]), "w_ff2": f(np.asarray(w_ff2)[0]),
    }
    in_maps = []
    for r in range(8):
        b, half = r // 2, r % 2
        own = OWN[half]
        xb = x[b]
        xo = np.zeros((1152, 2048), np.float32)
        hv = np.zeros((128, 8, 16), np.float32)
        for j, g in enumerate(own):
            xo[j * 128:(j + 1) * 128] = xb[g * 128:(g + 1) * 128]
            if g > 0:
                xo[1024 + j * 16:1024 + (j + 1) * 16] = xb[g * 128 - 16:g * 128]
                hv[:, j, :] = 1.0
        invc = np.zeros((128, 4, 128), np.float32)
        for g, w in enumerate([2, 4, 8, 16]):
            if half == 0:
                cnt = np.minimum(np.arange(128) + 1, w).astype(np.float32)
            else:
                cnt = np.full(128, w, np.float32)
            invc[:, g, :] = (1.0 / cnt)[None, :]
        m = dict(shared)
        m.update({
            "xa": xb, "xo": xo, "csil": f(c[b].reshape(16, 128).T), "hv16": hv.reshape(128, 128),
            "invc": invc.reshape(128, 512), "c_negmask": _negmask(half),
        })
        in_maps.append(m)
    res = run_bass_kernel_spmd(nc, in_maps, core_ids=list(range(8)))
    outp = np.empty((4, 2048, 2048), np.float32)
    for r in range(8):
        b, half = r // 2, r % 2
        o = res.results[r]["out"]
        for j, g in enumerate(OWN[half]):
            outp[b, g * 128:(g + 1) * 128] = o[j * 128:(j + 1) * 128]
    return outp
```

```python
from contextlib import ExitStack

import numpy as np
import concourse.bass as bass
import concourse.mybir as mybir
from concourse.bass_utils import run_bass_kernel_spmd

F32 = mybir.dt.float32
BF16 = mybir.dt.bfloat16
AF = mybir.ActivationFunctionType
ALU = mybir.AluOpType

import os
STAGE = int(os.environ.get("KSTAGE", "99"))
SUB = int(os.environ.get("KSUB", "99"))
VAR = int(os.environ.get("KVAR", "0"))


class _Stop(Exception):
    pass


ENGS = ["pe", "act", "dve", "pool", "sp"]
NEG = -30000.0
EPS = 1e-6
OWN = [[0, 3, 4, 7, 8, 11, 12, 15], [1, 2, 5, 6, 9, 10, 13, 14]]


class Buf:
    __slots__ = ("name", "w", "r")

    def __init__(self, name):
        self.name = name
        self.w = {}
        self.r = {}

    def set_w(self, ev):
        self.w = {ev[0]: ev[1]}


class Prog:
    def __init__(self, nc):
        self.nc = nc
        self.streams = {e: [] for e in ENGS}
        self.cnt = {}
        self.waited = {}
        self.sems = {}
        self.semnames = []
        for e in ENGS:
            self.new_sem("E_" + e)

    def new_sem(self, key):
        self.semnames.append(key)
        self.cnt[key] = 0
        return key

    def op(self, eng, fn, reads=(), writes=(), sem=None, inc=1, pwrites=()):
        need = {}

        def add(k, v):
            if need.get(k, 0) < v:
                need[k] = v
        for b in reads:
            for k, v in b.w.items():
                add(k, v)
        for b in writes:
            for k, v in b.w.items():
                add(k, v)
            for k, v in b.r.items():
                add(k, v)
        for b in pwrites:
            for k, v in b.r.items():
                add(k, v)
        waits = []
        for k, v in need.items():
            if self.waited.get((eng, k), 0) < v:
                self.waited[(eng, k)] = v
                waits.append((k, v))
        key = sem if sem is not None else "E_" + eng
        self.cnt[key] += inc
        ev = (key, self.cnt[key])
        sems = self.sems

        def run(e, waits=waits, fn=fn, key=key, inc=inc):
            for k, v in waits:
                e.wait_ge(sems[k], v)
            ins = fn(e)
            ins.then_inc(sems[key], inc)
        self.streams[eng].append(run)
        for b in writes:
            b.w = {ev[0]: ev[1]}
            b.r = {}
        for b in pwrites:
            if b.w.get(ev[0], 0) < ev[1]:
                b.w[ev[0]] = ev[1]
        for b in reads:
            if b.r.get(ev[0], 0) < ev[1]:
                b.r[ev[0]] = ev[1]
        return ev

    def barrier(self):
        snap = {("E_" + e): self.cnt["E_" + e] for e in ENGS}
        sems = self.sems
        for e in ENGS:
            waits = []
            for k, v in snap.items():
                if v > 0 and self.waited.get((e, k), 0) < v:
                    self.waited[(e, k)] = v
                    waits.append((k, v))
            if waits:
                def run(h, waits=waits):
                    for k, v in waits:
                        h.wait_ge(sems[k], v)
                self.streams[e].append(run)

    def wait_all(self, eng, keys):
        sems = self.sems
        waits = [(k, self.cnt[k]) for k in keys if self.cnt[k] > 0]

        def run(h, waits=waits):
            for k, v in waits:
                h.wait_ge(sems[k], v)
        self.streams[eng].append(run)

    def build(self):
        nc = self.nc
        with ExitStack() as st:
            for k in self.semnames:
                self.sems[k] = st.enter_context(nc.semaphore(k))
            block = st.enter_context(nc.Block())

            @block.tensor
            def _(h):
                for f in self.streams["pe"]:
                    f(h)

            @block.scalar
            def _(h):
                for f in self.streams["act"]:
                    f(h)

            @block.vector
            def _(h):
                for f in self.streams["dve"]:
                    f(h)

            @block.gpsimd
            def _(h):
                for f in self.streams["pool"]:
                    f(h)

            @block.sync
            def _(h):
                for f in self.streams["sp"]:
                    f(h)


def build_program():
    nc = bass.Bass("TRN2", target_bir_lowering=False)
    P = Prog(nc)

    def din(name, shape):
        return nc.dram_tensor(name, list(shape), F32, kind="ExternalInput").ap()

    xa = din("xa", [2048, 2048])
    xo = din("xo", [1152, 2048])
    csil = din("csil", [128, 16])
    bada = din("bada", [128, 96])
    n1w = din("n1w", [128, 16])
    n2w = din("n2w", [128, 16])
    qkw = din("qkw", [128, 2])
    pscale = din("pscale", [128, 8])
    hv16 = din("hv16", [128, 128])
    invc = din("invc", [128, 512])
    c_ident = din("c_ident", [128, 128])
    c_negtri = din("c_negtri", [128, 128])
    c_ones = din("c_ones", [128, 128])
    c_E = din("c_E", [128, 256])
    c_S = din("c_S", [16, 2048])
    c_negmask = din("c_negmask", [128, 1024])
    w_ada = din("w_ada", [2048, 12288])
    w_in = din("w_in", [2048, 8192])
    w_pool = din("w_pool", [4, 256, 256])
    w_a_up = din("w_a_up", [1024, 2048])
    w_b_up = din("w_b_up", [1024, 2048])
    w_o = din("w_o", [2048, 2048])
    w_ff1 = din("w_ff1", [2048, 8192])
    w_ff2 = din("w_ff2", [8192, 2048])
    out = nc.dram_tensor("out", [1024, 2048], F32, kind="ExternalOutput").ap()

    with ExitStack() as st:
        def sb(name, shape, dt):
            return st.enter_context(nc.sbuf_tensor(name, list(shape), dt))

        ARENA_KB = 159
        arena = sb("arena", [128, ARENA_KB * 512], BF16)

        def av(kb_off, kb_len, dt=BF16):
            a = arena[:, int(kb_off * 512):int((kb_off + kb_len) * 512)]
            return a.bitcast(F32) if dt == F32 else a

        slab_t = [sb("slab0", [128, 16, 512], BF16), sb("slab1", [128, 16, 512], BF16)]
        ident = sb("ident", [128, 128], BF16)
        negtri = sb("negtri", [128, 128], BF16)
        ones_bf = sb("ones_bf", [128, 128], BF16)
        negones = sb("negones", [128, 128], BF16)
        Et = sb("Et", [128, 16, 16], BF16)
        St = sb("St", [16, 16, 128], BF16)
        negmask = sb("negmask", [128, 4, 256], BF16)
        invct = sb("invct", [128, 4, 128], F32)
        hvt = sb("hvt", [128, 8, 16], F32)
        wpool = sb("wpool", [128, 4, 2, 256], BF16)
        csil_t = sb("csil_t", [128, 16], F32)
        sig_t = sb("sig_t", [128, 16], F32)
        scT = sb("scT", [128, 16], BF16)
        bada_t = sb("bada_t", [128, 96], F32)
        modfm = sb("modfm", [128, 96], F32)
        n1w_t = sb("n1w_t", [128, 16], F32)
        n2w_t = sb("n2w_t", [128, 16], F32)
        A1 = sb("A1", [128, 16], F32)
        A2 = sb("A2", [128, 16], F32)
        qkw_t = sb("qkw_t", [128, 2], F32)
        qw_s = sb("qw_s", [128, 1], F32)
        psc_t = sb("psc_t", [128, 8], F32)
        gh_bf = sb("gh_bf", [128, 16], BF16)
        gh32 = sb("gh32", [128, 16], F32)
        gl32 = sb("gl32", [128, 16], F32)
        Dh = [sb("Dh0", [128, 128], BF16), sb("Dh1", [128, 128], BF16)]
        Dl = [sb("Dl0", [128, 128], BF16), sb("Dl1", [128, 128], BF16)]
        DB = [Buf("D0"), Buf("D1")]
        dctr = [0]
        sst = [sb("ss0", [128, 4], F32), sb("ss1", [128, 4], F32)]
        ps = [st.enter_context(nc.psum_tensor("ps%d" % i, [128, 512], F32)) for i in range(8)]
        psB = [Buf("ps%d" % i) for i in range(8)]

        P.new_sem("ld_c_pool")
        P.new_sem("ld_c_sp")
        cbufs = {}
        cq = {}

        def cload(eng, dst, src, name):
            b = Buf(name)
            cbufs[name] = b
            cq[name] = eng
            P.op(eng, lambda e: e.dma_start(out=dst, in_=src), writes=[b], sem="ld_c_" + eng, inc=16)
            return b

        cload("pool", ident[:], c_ident, "ident")
        cload("pool", negtri[:], c_negtri, "negtri")
        cload("pool", ones_bf[:], c_ones, "ones_bf")
        cload("pool", Et[:], c_E.rearrange("p (a b) -> p a b", a=16), "E")
        cload("pool", St[:], c_S.rearrange("p (a b) -> p a b", a=16), "S")
        cload("pool", negmask[:], c_negmask.rearrange("p (a b) -> p a b", a=4), "negmask")
        cload("pool", wpool[:], w_pool.rearrange("g (c p) e -> p g c e", p=128), "wpool")
        cload("sp", invct[:], invc.rearrange("p (a b) -> p a b", a=4), "invc")
        cload("sp", hvt[:], hv16.rearrange("p (a b) -> p a b", a=8), "hv")
        cload("sp", csil_t[:], csil, "csil")
        cload("sp", bada_t[:], bada, "bada")
        cload("sp", n1w_t[:], n1w, "n1w")
        cload("sp", n2w_t[:], n2w, "n2w")
        cload("sp", qkw_t[:], qkw, "qkw")
        cload("sp", psc_t[:], pscale, "psc")
        for name, b in cbufs.items():
            b.set_w(("ld_c_" + cq[name], P.cnt["ld_c_" + cq[name]]))
        CB = cbufs
        CB["negones"] = Buf("negones")

        for i in range(2):
            P.new_sem("ld_s%dlo" % i)
            P.new_sem("ld_s%dhi" % i)
        slabB = [(Buf("s0lo"), Buf("s0hi")), (Buf("s1lo"), Buf("s1hi"))]

        def rows16(w, r0, c0, nk=16):
            return w[r0:r0 + nk * 128, c0:c0 + 512].rearrange("(k p) c -> p k c", p=128)

        sched = []
        for n in list(range(0, 8)):
            sched.append([("full", rows16(w_ada, 0, n * 512))])
        for s in [0, 1, 2, 3]:
            sched.append([("full", rows16(w_in, 0, s * 512))])
        for _half in range(2):
            for s in [4, 5, 6, 7]:
                sched.append([("full", rows16(w_in, 0, s * 512))])
        ADA_LATE = list(range(12, 20)) + [8, 9, 10, 11, 20, 21, 22, 23]
        for n in ADA_LATE:
            sched.append([("full", rows16(w_ada, 0, n * 512))])
        for s in range(4):
            sched.append([("full", rows16(w_in, 0, 4096 + s * 512))])
            sched.append([("lo", rows16(w_a_up, 0, s * 512, 8)), ("hi", rows16(w_b_up, 0, s * 512, 8))])
            sched.append([("full", rows16(w_in, 0, 6144 + s * 512))])
        for s in range(4):
            sched.append([("full", rows16(w_o, 0, s * 512))])
        for q in range(4):
            for s in range(4):
                sched.append([("full", rows16(w_ff1, 0, q * 2048 + s * 512))])
            for s in range(4):
                sched.append([("full", rows16(w_ff2, q * 2048, s * 512))])

        wstate = {"next_issue": 0, "next_use": 0, "slot_of": {}}
        free_slots = [0, 1]

        def issue_next():
            while free_slots and wstate["next_issue"] < len(sched):
                i = wstate["next_issue"]
                slot = free_slots.pop(0)
                wstate["slot_of"][i] = slot
                lo, hi = slabB[slot]
                for (half, src) in sched[i]:
                    if half == "full":
                        P.op("pool", lambda e, slot=slot, src=src: e.dma_start(out=slab_t[slot][:], in_=src),
                             writes=[lo, hi], sem="ld_s%dlo" % slot, inc=16)
                    elif half == "lo":
                        P.op("pool", lambda e, slot=slot, src=src: e.dma_start(out=slab_t[slot][:, 0:8, :], in_=src),
                             writes=[lo], sem="ld_s%dlo" % slot, inc=16)
                    else:
                        P.op("pool", lambda e, slot=slot, src=src: e.dma_start(out=slab_t[slot][:, 8:16, :], in_=src),
                             writes=[hi], sem="ld_s%dhi" % slot, inc=16)
                wstate["next_issue"] += 1

        def wnext():
            i = wstate["next_use"]
            wstate["next_use"] += 1
            if i not in wstate["slot_of"]:
                issue_next()
            slot = wstate["slot_of"][i]
            return slab_t[slot], list(slabB[slot]), slot

        def wrelease(slot):
            free_slots.append(slot)
            issue_next()

        issue_next()

        def mm_group(out_ap, pairs, reads, writes):
            n = len(pairs)

            def fn(e):
                ins = None
                for i, (l, r) in enumerate(pairs):
                    ins = e.matmul(out_ap, lhsT=l, rhs=r, start=(i == 0), stop=(i == n - 1))
                return ins
            return P.op("pe", fn, reads=reads, writes=writes)

        nrm_ctr = [0]

        def norm_block(src_ap, src_bufs, xt_bufs, hns, dst, dstB, A_t, B_t, modB, from_dram):
            i = nrm_ctr[0]
            nrm_ctr[0] += 1
            ss = sst[i % 2]
            ssB = ssBs[i % 2]
            hn_t, hnB = hns[i % len(hns)]
            if from_dram:
                xt_t, xtB, ldk = xt_bufs[i % len(xt_bufs)]
                P.op("sp", lambda e: e.dma_start(out=xt_t, in_=src_ap), writes=[xtB], sem=ldk, inc=16)
                xin, xinB = xt_t, [xtB]
            else:
                xin, xinB = src_ap, src_bufs
            P.op("dve", lambda e: e.memset(ss[:, 0:1], 0.0), writes=[ssB])
            P.op("act", lambda e: e.activation(out=hn_t, in_=xin, func=AF.Square, accum_out=ss[:, 0:1]),
                 reads=xinB, writes=[hnB, ssB])
            P.op("act", lambda e: e.activation(out=ss[:, 1:2], in_=ss[:, 0:1], func=AF.Ln, scale=1.0 / 2048, bias=EPS),
                 reads=[ssB], writes=[ssB])
            P.op("act", lambda e: e.activation(out=ss[:, 2:3], in_=ss[:, 1:2], func=AF.Exp, scale=-0.5), reads=[ssB], writes=[ssB])
            P.op("dve", lambda e: e.tensor_scalar(out=hn_t, in0=xin, scalar1=ss[:, 2:3], scalar2=None, op0=ALU.mult),
                 reads=xinB + [ssB], writes=[hnB])
            b0, b1 = [(4, 5), (6, 7)][i % 2]
            pv = [ps[b0][:].bitcast(BF16), ps[b1][:].bitcast(BF16)]

            def back():
                def tr(e):
                    ins = None
                    for c in range(16):
                        ins = e.transpose(pv[c // 8][:, (c % 8) * 128:(c % 8 + 1) * 128], hn_t[:, c * 128:(c + 1) * 128], ident[:])
                    return ins
                P.op("pe", tr, reads=[hnB, CB["ident"]], writes=[psB[b0], psB[b1]])
                for c in range(16):
                    src = pv[c // 8][:, (c % 8) * 128:(c % 8 + 1) * 128]
                    if c < 8:
                        P.op("act", lambda e, c=c, src=src: e.activation(out=dst[:, c, :], in_=src, func=AF.Identity,
                                                                         bias=B_t[:, c:c + 1], scale=A_t[:, c:c + 1]),
                             reads=[psB[b0], modB], pwrites=[dstB])
                    else:
                        P.op("dve", lambda e, c=c, src=src: e.tensor_scalar(out=dst[:, c, :], in0=src, scalar1=A_t[:, c:c + 1],
                                                                            scalar2=B_t[:, c:c + 1], op0=ALU.mult, op1=ALU.add),
                             reads=[psB[b1], modB], pwrites=[dstB])
            return back

        class NormPipe:
            def __init__(self):
                self.pending = None

            def push(self, args):
                bk = norm_block(*args)
                if self.pending is not None:
                    self.pending()
                self.pending = bk

            def flush(self):
                if self.pending is not None:
                    self.pending()
                self.pending = None

        def norm_seq(arglist):
            npipe = NormPipe()
            for args in arglist:
                npipe.push(args)
            npipe.flush()

        ssBs = [Buf("ss0"), Buf("ss1")]
        qk_ctr = [0]

        def qknorm(psv, psbuf, wcol, wB, dest, destB, tset):
            sq, raw, rt, rinv, tB = tset
            sb_i = 3
            LV = VAR if VAR >= 10 else 99
            if VAR == 15:
                P.op("act", lambda e: e.activation(out=sq, in_=psv, func=AF.Square), reads=[psbuf], writes=[tB["sq"]])
                P.op("dve", lambda e: e.tensor_copy(out=dest, in_=psv), reads=[psbuf], writes=[destB])
                return
            if VAR == 16:
                P.op("dve", lambda e: e.tensor_copy(out=raw, in_=psv), reads=[psbuf], writes=[tB["raw"]])
                P.op("dve", lambda e: e.tensor_copy(out=dest, in_=raw), reads=[tB["raw"]], writes=[destB])
                return
            P.op("dve", lambda e: e.tensor_copy(out=raw, in_=psv), reads=[psbuf], writes=[tB["raw"]])
            P.op("act", lambda e: e.activation(out=sq, in_=raw, func=AF.Square), reads=[tB["raw"]], writes=[tB["sq"]])
            if LV >= 12:
                mm_group(ps[sb_i][:, :], [(ones_bf[:], sq)], reads=[tB["sq"], CB["ones_bf"]], writes=[psB[sb_i]])
                P.op("act", lambda e: e.activation(out=rt, in_=ps[sb_i][:, :], func=AF.Ln, scale=1.0 / 128, bias=EPS),
                     reads=[psB[sb_i]], writes=[tB["rt"]])
            if LV >= 13:
                P.op("act", lambda e: e.activation(out=rinv, in_=rt, func=AF.Exp, scale=-0.5), reads=[tB["rt"]], writes=[tB["rinv"]])
            if LV >= 14:
                P.op("dve", lambda e: e.scalar_tensor_tensor(out=dest, in0=raw, scalar=wcol, in1=rinv, op0=ALU.mult, op1=ALU.mult),
                     reads=[tB["raw"], tB["rinv"], wB], writes=[destB])
            else:
                P.op("dve", lambda e: e.tensor_copy(out=dest, in_=raw), reads=[tB["raw"]], writes=[destB])

        def mk_tset(kb):
            sq = av(kb, 1)
            raw = av(kb + 1, 2, F32)
            rt = av(kb + 3, 2, F32)
            rinv = av(kb + 5, 2, F32)
            return (sq, raw, rt, rinv, {k: Buf(k) for k in ["sq", "raw", "rt", "rinv"]})

        truncated = False
        try:
            modB = Buf("modfm")
            P.op("act", lambda e: e.activation(out=sig_t[:], in_=csil_t[:], func=AF.Sigmoid), reads=[CB["csil"]], writes=[modB])
            scB = Buf("scT")
            P.op("dve", lambda e: e.tensor_tensor(out=scT[:], in0=sig_t[:], in1=csil_t[:], op=ALU.mult),
                 reads=[modB, CB["csil"]], writes=[scB])
            P.op("dve", lambda e: e.tensor_scalar(out=qw_s[:], in0=qkw_t[:, 0:1], scalar1=float(128.0 ** -0.5), scalar2=None,
                                                  op0=ALU.mult), reads=[CB["qkw"]], writes=[CB["qkw"]])
            ada_ctr = [0]

            def ada_cols(n, bank=None):
                j = ada_ctr[0] % 2 if bank is None else bank
                ada_ctr[0] += 1
                sl, sB, slot = wnext()

                def fn(e, sl=sl, j=j):
                    ins = None
                    for c in range(4):
                        for k in range(16):
                            ins = e.matmul(ps[j][:, c:c + 1], lhsT=sl[:, k, c * 128:(c + 1) * 128], rhs=scT[:, k:k + 1],
                                           start=(k == 0), stop=(k == 15))
                    return ins
                P.op("pe", fn, reads=sB + [scB], writes=[psB[j]])
                wrelease(slot)
                P.op("dve", lambda e, n=n, j=j: e.tensor_tensor(out=modfm[:, 4 * n:4 * n + 4], in0=ps[j][:, 0:4],
                                                                in1=bada_t[:, 4 * n:4 * n + 4], op=ALU.add),
                     reads=[psB[j], CB["bada"]], writes=[modB])

            for n in list(range(0, 8)):
                ada_cols(n)
            P.op("dve", lambda e: e.scalar_tensor_tensor(out=A1[:], in0=modfm[:, 16:32], scalar=1.0, in1=n1w_t[:],
                                                         op0=ALU.add, op1=ALU.mult), reads=[modB, CB["n1w"]], writes=[modB])
            B1 = modfm[:, 0:16]
            B2 = modfm[:, 48:64]

            if STAGE == 1:
                P.barrier()
                raise _Stop()
            q_fm = av(0, 16).rearrange("p (h t) -> p h t", h=8)
            mixed = av(16, 16).rearrange("p (c t) -> p c t", c=8)
            h_own = av(32, 36).rearrange("p (k t) -> p k t", k=16)
            xtb = []
            for i in range(2):
                k = P.new_sem("ld_xt%d" % i)
                xtb.append((av(68 + 8 * i, 8, F32), Buf("xt%d" % i), k))
            hns1 = [(av(84, 4), Buf("hn_a")), (av(120, 4), Buf("hn_b"))]
            tsets = [mk_tset(88), mk_tset(95)]
            ucat = av(102, 4.5, F32).rearrange("p (b t) -> p b t", b=8)
            tmpA = av(106.5, 4.5, F32).rearrange("p (b t) -> p b t", b=8)
            tmpB = av(111, 4.5, F32).rearrange("p (b t) -> p b t", b=8)
            pooled = av(115.5, 4).rearrange("p (c t) -> p c t", c=2)
            tmp0 = av(119.5, 0.5, F32)
            qB, mixB = Buf("q"), Buf("mixed")
            hownBs = [Buf("h_own_t0"), Buf("h_own_t1"), Buf("h_own_halo")]
            ucatB, tmpAB, tmpBB, pooledB, tmp0B = Buf("ucat"), Buf("tmpA"), Buf("tmpB"), Buf("pooled"), Buf("tmp0")

            norm_seq([(xo[tb * 128:(tb + 1) * 128, :], None, xtb, hns1,
                       h_own[:, :, tb * 128:(tb + 1) * 128], hownBs[tb // 4], A1, B1, modB, True) for tb in range(9)])

            if STAGE == 2 and SUB == 1:
                P.barrier()
                raise _Stop()
            pbank = [0]

            def nbank(lo=0, hi=3):
                b = lo + pbank[0] % (hi - lo)
                pbank[0] += 1
                return b

            def mk_uset(kb, tag):
                return {"ucat": av(kb, 4.5, F32).rearrange("p (b t) -> p b t", b=8),
                        "tmpA": av(kb + 4.5, 4.5, F32).rearrange("p (b t) -> p b t", b=8),
                        "tmpB": av(kb + 9, 4.5, F32).rearrange("p (b t) -> p b t", b=8),
                        "tmp0": av(kb + 13.5, 0.5, F32),
                        "ucatB": Buf("ucat" + tag), "tmpAB": Buf("tmpA" + tag), "tmpBB": Buf("tmpB" + tag), "tmp0B": Buf("tmp0" + tag)}
            usets = [{"ucat": ucat, "tmpA": tmpA, "tmpB": tmpB, "tmp0": tmp0,
                      "ucatB": ucatB, "tmpAB": tmpAB, "tmpBB": tmpBB, "tmp0B": tmp0B}, mk_uset(124, "2")]
            pooledS = [(pooled, pooledB), (av(138, 4).rearrange("p (c t) -> p c t", c=2), Buf("pooled2"))]
            pending_mix = [None]

            def emit_mix(g):
                pl, plB = pooledS[g % 2]
                for ec in range(2):
                    for th in range(2):
                        b = nbank()
                        mm_group(ps[b][:, :], [(wpool[:, g, k, ec * 128:(ec + 1) * 128], pl[:, k, th * 512:(th + 1) * 512])
                                               for k in range(2)], reads=[CB["wpool"], plB], writes=[psB[b]])
                        P.op("act", lambda e, b=b, g=g, ec=ec, th=th: e.activation(
                            out=mixed[:, 2 * g + ec, th * 512:(th + 1) * 512], in_=ps[b][:, :], func=AF.Identity,
                            scale=psc_t[:, 2 * g + ec:2 * g + ec + 1]), reads=[psB[b], CB["psc"]], writes=[mixB])

            for s in range(2):
                sl, sB, slot = wnext()
                for uc4 in range(4):
                    uc = s * 4 + uc4
                    g = uc // 2
                    cc = uc % 2
                    w = 2 ** (g + 1)
                    U = usets[uc % 2]
                    uct, uctB = U["ucat"], U["ucatB"]
                    pl, plB = pooledS[g % 2]
                    bm = []
                    for th in range(2):
                        b = nbank()
                        mm_group(ps[b][:, :], [(sl[:, k, uc4 * 128:(uc4 + 1) * 128], h_own[:, k, th * 512:(th + 1) * 512])
                                               for k in range(16)], reads=sB + [hownBs[th]], writes=[psB[b]])
                        bm.append(b)
                    bh = nbank()
                    mm_group(ps[bh][:, 0:128], [(sl[:, k, uc4 * 128:(uc4 + 1) * 128], h_own[:, k, 1024:1152]) for k in range(16)],
                             reads=sB + [hownBs[2]], writes=[psB[bh]])
                    P.op("act", lambda e, b=bm[0], uct=uct: e.activation(out=uct[:, 0:4, 16:144],
                                                                         in_=ps[b][:, :].rearrange("p (b t) -> p b t", b=4), func=AF.Identity),
                         reads=[psB[bm[0]]], pwrites=[uctB])
                    P.op("act", lambda e, b=bm[1], uct=uct: e.activation(out=uct[:, 4:8, 16:144],
                                                                         in_=ps[b][:, :].rearrange("p (b t) -> p b t", b=4), func=AF.Identity),
                         reads=[psB[bm[1]]], pwrites=[uctB])
                    P.op("dve", lambda e, b=bh, uct=uct: e.tensor_tensor(out=uct[:, :, 0:16], in0=ps[b][:, 0:128].rearrange("p (b t) -> p b t", b=8),
                                                                         in1=hvt[:], op=ALU.mult), reads=[psB[bh], CB["hv"]], pwrites=[uctB])
                    if pending_mix[0] is not None:
                        emit_mix(pending_mix[0])
                        pending_mix[0] = None
                    tA, tAB, tB_, tBB = U["tmpA"], U["tmpAB"], U["tmpB"], U["tmpBB"]
                    P.op("dve", lambda e, tA=tA, uct=uct: e.tensor_tensor(out=tA[:, :, 1:144], in0=uct[:, :, 1:144], in1=uct[:, :, 0:143], op=ALU.add),
                         reads=[uctB], writes=[tAB])
                    cur, curB, oth, othB = tA, tAB, tB_, tBB
                    sh = 2
                    while sh < w:
                        lo_ = 2 * sh - 1
                        P.op("dve", lambda e, cur=cur, oth=oth, lo_=lo_, sh=sh: e.tensor_tensor(
                            out=oth[:, :, lo_:144], in0=cur[:, :, lo_:144], in1=cur[:, :, lo_ - sh:144 - sh], op=ALU.add),
                            reads=[curB], writes=[othB])
                        cur, curB, oth, othB = oth, othB, cur, curB
                        sh *= 2
                    t0_, t0B_ = U["tmp0"], U["tmp0B"]
                    P.op("dve", lambda e, cur=cur, w=w, cc=cc, pl=pl, uct=uct: e.scalar_tensor_tensor(
                        out=pl[:, cc, 128:1024].rearrange("p (b t) -> p b t", b=7), in0=cur[:, 1:8, 16:144], scalar=1.0 / w,
                        in1=uct[:, 1:8, 16:144], op0=ALU.mult, op1=ALU.subtract), reads=[curB, uctB], pwrites=[plB])
                    P.op("dve", lambda e, cur=cur, g=g, t0_=t0_: e.tensor_tensor(out=t0_, in0=cur[:, 0, 16:144], in1=invct[:, g, :], op=ALU.mult),
                         reads=[curB, CB["invc"]], writes=[t0B_])
                    P.op("dve", lambda e, cc=cc, pl=pl, t0_=t0_, uct=uct: e.tensor_tensor(out=pl[:, cc, 0:128], in0=t0_, in1=uct[:, 0, 16:144],
                                                                                          op=ALU.subtract),
                         reads=[t0B_, uctB], pwrites=[plB])
                    if cc == 1:
                        pending_mix[0] = g
                wrelease(slot)
            if pending_mix[0] is not None:
                emit_mix(pending_mix[0])
                pending_mix[0] = None
            if STAGE == 2 and SUB == 2:
                P.barrier()
                raise _Stop()
            qc = [0]
            pend = None
            for s in range(2):
                sl, sB, slot = wnext()
                for h4 in range(4):
                    h = s * 4 + h4
                    for th in range(2):
                        b = nbank()
                        mm_group(ps[b][:, :], [(sl[:, k, h4 * 128:(h4 + 1) * 128], h_own[:, k, th * 512:(th + 1) * 512])
                                               for k in range(16)], reads=sB + [hownBs[th]], writes=[psB[b]])
                        if pend is not None:
                            qknorm(*pend)
                        pend = (ps[b][:, :], psB[b], qw_s[:, 0:1], CB["qkw"], q_fm[:, h, th * 512:(th + 1) * 512], qB,
                                tsets[qc[0] % 2])
                        qc[0] += 1
                wrelease(slot)
            qknorm(*pend)

            P.barrier()
            if STAGE == 2:
                raise _Stop()
            k_fm = av(32, 32).rearrange("p (h t) -> p h t", h=8)
            v_tm = av(64, 32).rearrange("p (b c) -> p b c", b=16)
            h_half = av(96, 32).rearrange("p (k t) -> p k t", k=16)
            kx = P.new_sem("ld_xta")
            xta = [(av(128, 8, F32), Buf("xta"), kx)]
            hns2 = [(av(136, 4), Buf("hn2a")), (av(140, 4), Buf("hn2b"))]
            tsets_a = [mk_tset(144), mk_tset(151)]
            kc = [0]
            kB, vB = Buf("k"), Buf("v")
            hhBs = [Buf("h_half_%d" % i) for i in range(8)]

            def nargs(half, tb):
                return (xa[(half * 8 + tb) * 128:(half * 8 + tb + 1) * 128, :], None, xta, hns2,
                        h_half[:, :, tb * 128:(tb + 1) * 128], hhBs[tb], A1, B1, modB, True)
            npipe = NormPipe()
            for tb in range(8):
                npipe.push(nargs(0, tb))
            npipe.flush()
            for half in range(2):
                pend = None
                for s in range(2):
                    sl, sB, slot = wnext()
                    for th in range(2):
                        for h4 in range(4):
                            h = s * 4 + h4
                            b = nbank()
                            mm_group(ps[b][:, :], [(sl[:, k, h4 * 128:(h4 + 1) * 128], h_half[:, k, th * 512:(th + 1) * 512])
                                                   for k in range(16)], reads=sB + hhBs[4 * th:4 * th + 4], writes=[psB[b]])
                            t0 = half * 1024 + th * 512
                            if pend is not None:
                                qknorm(*pend)
                            pend = (ps[b][:, :], psB[b], qkw_t[:, 1:2], CB["qkw"], k_fm[:, h, t0:t0 + 512], kB, tsets_a[kc[0] % 2])
                            kc[0] += 1
                    wrelease(slot)
                qknorm(*pend)
                for s in range(2):
                    sl, sB, slot = wnext()
                    for tb in range(8):
                        b = nbank()
                        mm_group(ps[b][:, :], [(h_half[:, k, tb * 128:(tb + 1) * 128], sl[:, k, :]) for k in range(16)],
                                 reads=sB + [hhBs[tb]], writes=[psB[b]])
                        gtb = half * 8 + tb
                        if tb % 2 == 0:
                            P.op("act", lambda e, b=b, gtb=gtb, s=s: e.activation(out=v_tm[:, gtb, s * 512:(s + 1) * 512], in_=ps[b][:, :],
                                                                                 func=AF.Identity), reads=[psB[b]], pwrites=[vB])
                        else:
                            P.op("dve", lambda e, b=b, gtb=gtb, s=s: e.tensor_copy(out=v_tm[:, gtb, s * 512:(s + 1) * 512], in_=ps[b][:, :]),
                                 reads=[psB[b]], pwrites=[vB])
                        if half == 0 and s == 1:
                            npipe.push(nargs(1, tb))
                    wrelease(slot)
                if half == 0:
                    npipe.flush()

            P.barrier()
            if STAGE == 3:
                raise _Stop()
            o_fm = av(96, 16).rearrange("p (h t) -> p h t", h=8)
            oB = Buf("o_fm")

            def ring(kb0, kb_each, n, dt, name):
                return [(av(kb0 + j * kb_each, kb_each, dt), Buf("%s%d" % (name, j))) for j in range(n)]
            e_r = ring(112, 2, 3, F32, "e")
            l_r = ring(118, 1, 3, BF16, "l")
            a_r = ring(121, 1, 3, BF16, "a")
            R32 = ring(124, 2, 2, F32, "R32")
            Rb = ring(128, 1, 2, BF16, "Rb")
            zeros_bf = av(130, 1)
            zerosB = Buf("zeros")
            P.op("dve", lambda e: e.memset(zeros_bf, 0.0), writes=[zerosB])
            P.op("dve", lambda e: e.tensor_scalar(out=negones[:], in0=ones_bf[:], scalar1=-1.0, scalar2=None, op0=ALU.mult),
                 reads=[CB["ones_bf"]], writes=[CB["negones"]])
            steps = []
            for hh in range(4):
                for i in range(4):
                    n = 4 * i + 4
                    for j, p in enumerate(range(n - 1, -1, -1)):
                        steps.append({"hs": (hh, hh + 4), "i": i, "n": n, "p": p, "j": j})
            zr = [kB, qB, CB["ident"], CB["negmask"]]

            def zmm(e, o_, h, i, p, first, last):
                msk = p >= 4 * i
                ins = e.matmul(o_, lhsT=k_fm[:, h, p * 128:(p + 1) * 128], rhs=q_fm[:, h, i * 256:(i + 1) * 256],
                               start=first, stop=(last and not msk))
                if msk:
                    ins = e.matmul(o_, lhsT=ident[:], rhs=negmask[:, p - 4 * i, :], start=False, stop=last)
                return ins

            def st1(m):
                s_ = steps[m]
                i, p = s_["i"], s_["p"]
                zb = m % 2

                def fn(e):
                    ins = None
                    for sl_, h in enumerate(s_["hs"]):
                        ins = zmm(e, ps[zb][:, sl_ * 256:(sl_ + 1) * 256], h, i, p, True, True)
                    return ins
                P.op("pe", fn, reads=zr, writes=[psB[zb]])

            def st2(m):
                zb = m % 2
                ee, eeB = e_r[m % 3]
                P.op("act", lambda e: e.activation(out=ee, in_=ps[zb][:, :], func=AF.Exp), reads=[psB[zb]], writes=[eeB])

            def st3(m):
                ee, eeB = e_r[m % 3]
                l_, lB_ = l_r[m % 3]
                P.op("act", lambda e: e.activation(out=l_, in_=ee, func=AF.Ln, bias=1.0), reads=[eeB], writes=[lB_])

            def st4(m):
                s_ = steps[m]
                i, j, p = s_["i"], s_["j"], s_["p"]
                l_, lB_ = l_r[m % 3]
                sbk = 2 + m % 3
                r_old, r_oldB = R32[j % 2]
                r_new, r_newB = R32[(j + 1) % 2]
                rb_old, rb_oldB = Rb[j % 2]
                rb_new, rb_newB = Rb[(j + 1) % 2]

                def fn(e):
                    ins = None
                    for sl_, h in enumerate(s_["hs"]):
                        o_ = ps[sbk][:, sl_ * 256:(sl_ + 1) * 256]
                        zmm(e, o_, h, i, p, True, False)
                        ins = e.matmul(o_, lhsT=negtri[:], rhs=l_[:, sl_ * 256:(sl_ + 1) * 256], start=False, stop=(j == 0))
                        if j > 0:
                            ins = e.matmul(o_, lhsT=negones[:], rhs=rb_old[:, sl_ * 256:(sl_ + 1) * 256], start=False, stop=True)
                    return ins
                rd = zr + [lB_, CB["negtri"]] + ([rb_oldB, CB["negones"]] if j > 0 else [])
                P.op("pe", fn, reads=rd, writes=[psB[sbk]])
                if p > 0:
                    if j == 0:
                        P.op("dve", lambda e: e.tensor_tensor(out=r_new, in0=l_, in1=zeros_bf, op=ALU.add),
                             reads=[lB_, zerosB], writes=[r_newB])
                        P.op("pool", lambda e: e.tensor_tensor(out=rb_new, in0=l_, in1=zeros_bf, op=ALU.add),
                             reads=[lB_, zerosB], writes=[rb_newB])
                    else:
                        P.op("dve", lambda e: e.tensor_tensor(out=r_new, in0=r_old, in1=l_, op=ALU.add),
                             reads=[r_oldB, lB_], writes=[r_newB])
                        P.op("pool", lambda e: e.tensor_tensor(out=rb_new, in0=r_old, in1=l_, op=ALU.add),
                             reads=[r_oldB, lB_], writes=[rb_newB])

            def st5(m):
                a_, aB_ = a_r[m % 3]
                sbk = 2 + m % 3
                P.op("act", lambda e: e.activation(out=a_, in_=ps[sbk][:, :], func=AF.Exp), reads=[psB[sbk]], writes=[aB_])

            def st6(m):
                s_ = steps[m]
                i, p, n = s_["i"], s_["p"], s_["n"]
                a_, aB_ = a_r[m % 3]

                def fn(e):
                    ins = None
                    for sl_, h in enumerate(s_["hs"]):
                        ins = e.matmul(ps[6 + sl_][:, 0:256], lhsT=v_tm[:, p, h * 128:(h + 1) * 128],
                                       rhs=a_[:, sl_ * 256:(sl_ + 1) * 256], start=(p == n - 1), stop=(p == 0))
                    return ins
                P.op("pe", fn, reads=[vB, aB_], writes=[psB[6], psB[7]])
                if p == 0:
                    for sl_, h in enumerate(s_["hs"]):
                        P.op("dve", lambda e, sl_=sl_, h=h: e.tensor_copy(out=o_fm[:, h, i * 256:(i + 1) * 256],
                                                                          in_=ps[6 + sl_][:, 0:256]),
                             reads=[psB[6 + sl_]], pwrites=[oB])

            stages = [st1, st2, st3, st4, st5, st6]
            NS = len(steps)
            ada_late = list(ADA_LATE)
            for k in range(NS + len(stages) - 1):
                for d_, fn_ in enumerate(stages):
                    if 0 <= k - d_ < NS:
                        fn_(k - d_)
                if k % 10 == 4 and ada_late:
                    ada_cols(ada_late.pop(0), bank=5)
            while ada_late:
                ada_cols(ada_late.pop(0), bank=5)

            P.barrier()
            if STAGE == 4:
                raise _Stop()
            h_own3 = av(32, 36).rearrange("p (k t) -> p k t", k=16)
            hown3Bs = [Buf("h_own3_t0"), Buf("h_own3_t1")]
            xtb3 = []
            for i in range(2):
                k = P.new_sem("ld_xu%d" % i)
                xtb3.append((av(68 + 8 * i, 8, F32), Buf("xu%d" % i), k))
            hns3 = [(av(84, 4), Buf("hn3a")), (av(92, 4), Buf("hn3b"))]
            merged = av(112, 32).rearrange("p (c t) -> p c t", c=16)
            mergedB = Buf("merged")
            sg = av(144, 8).rearrange("p (c t) -> p c t", c=4)
            sgB = Buf("sg")
            t1 = av(0, 16, F32).rearrange("p (c t) -> p c t", c=4)
            t1B = Buf("t1")
            t2s = [av(88, 2, F32), av(90, 2, F32)]
            t2B = [Buf("t2a"), Buf("t2b")]
            norm_seq([(xo[tb * 128:(tb + 1) * 128, :], None, xtb3, hns3,
                       h_own3[:, :, tb * 128:(tb + 1) * 128], hown3Bs[tb // 4], A1, B1, modB, True) for tb in range(8)])
            tc = [0]
            for s in range(4):
                slga, sBga, slotga = wnext()
                for th in range(2):
                    for c in range(4):
                        b = nbank()
                        mm_group(ps[b][:, :], [(slga[:, k, c * 128:(c + 1) * 128], h_own3[:, k, th * 512:(th + 1) * 512])
                                               for k in range(16)], reads=sBga + [hown3Bs[th]], writes=[psB[b]])
                        P.op("act", lambda e, b=b, c=c, th=th: e.activation(out=sg[:, c, th * 512:(th + 1) * 512], in_=ps[b][:, :],
                                                                           func=AF.Sigmoid), reads=[psB[b]], writes=[sgB])
                wrelease(slotga)
                slab_, sBab, slotab = wnext()
                for c in range(4):
                    for th in range(2):
                        b = nbank()
                        mm_group(ps[b][:, :], [(slab_[:, k, c * 128:(c + 1) * 128], mixed[:, k, th * 512:(th + 1) * 512])
                                               for k in range(8)], reads=[sBab[0], mixB], writes=[psB[b]])
                        P.op("dve", lambda e, b=b, c=c, th=th: e.tensor_tensor(out=t1[:, c, th * 512:(th + 1) * 512], in0=ps[b][:, :],
                                                                              in1=sg[:, c, th * 512:(th + 1) * 512], op=ALU.mult),
                             reads=[psB[b], sgB], writes=[t1B])
                slgb, sBgb, slotgb = wnext()
                for c in range(4):
                    for th in range(2):
                        b = nbank()
                        mm_group(ps[b][:, :], [(slgb[:, k, c * 128:(c + 1) * 128], h_own3[:, k, th * 512:(th + 1) * 512])
                                               for k in range(16)], reads=sBgb + [hown3Bs[th]], writes=[psB[b]])
                        P.op("act", lambda e, b=b, c=c, th=th: e.activation(out=sg[:, c, th * 512:(th + 1) * 512], in_=ps[b][:, :],
                                                                           func=AF.Sigmoid), reads=[psB[b]], writes=[sgB])
                wrelease(slotgb)
                for c in range(4):
                    for th in range(2):
                        b = nbank()
                        mm_group(ps[b][:, :], [(slab_[:, 8 + k, c * 128:(c + 1) * 128], o_fm[:, k, th * 512:(th + 1) * 512])
                                               for k in range(8)], reads=[sBab[1], oB], writes=[psB[b]])
                        ti = tc[0] % 2
                        tc[0] += 1
                        P.op("dve", lambda e, b=b, c=c, th=th, ti=ti: e.tensor_tensor(out=t2s[ti], in0=ps[b][:, :],
                                                                                     in1=sg[:, c, th * 512:(th + 1) * 512], op=ALU.mult),
                             reads=[psB[b], sgB], writes=[t2B[ti]])
                        P.op("pool", lambda e, c=c, th=th, ti=ti, s=s: e.tensor_tensor(out=merged[:, 4 * s + c, th * 512:(th + 1) * 512],
                                                                                      in0=t1[:, c, th * 512:(th + 1) * 512], in1=t2s[ti],
                                                                                      op=ALU.add), reads=[t1B, t2B[ti]], writes=[mergedB])
                wrelease(slotab)

            P.barrier()
            if STAGE == 5:
                raise _Stop()
            gate_bc = [av(96, 8, F32), av(104, 8, F32)]
            gateB = [Buf("g1"), Buf("g2")]
            A2B = Buf("A2")
            P.op("dve", lambda e: e.scalar_tensor_tensor(out=A2[:], in0=modfm[:, 64:80], scalar=1.0, in1=n2w_t[:],
                                                         op0=ALU.add, op1=ALU.mult), reads=[modB, CB["n2w"]], writes=[A2B])
            gB = Buf("gtmp")
            for gi, c0 in enumerate([32, 80]):
                P.op("dve", lambda e, c0=c0: e.tensor_copy(out=gh_bf[:], in_=modfm[:, c0:c0 + 16]), reads=[modB], writes=[gB])
                P.op("dve", lambda e: e.tensor_copy(out=gh32[:], in_=gh_bf[:]), reads=[gB], writes=[gB])
                P.op("dve", lambda e, c0=c0: e.tensor_tensor(out=gl32[:], in0=modfm[:, c0:c0 + 16], in1=gh32[:], op=ALU.subtract),
                     reads=[modB, gB], writes=[gB])
                for c4 in range(4):
                    b = nbank()
                    for cc in range(4):
                        c = c4 * 4 + cc
                        di = dctr[0] % 2
                        dctr[0] += 1
                        P.op("dve", lambda e, c=c, di=di: e.tensor_scalar(out=Dh[di][:], in0=ident[:], scalar1=gh32[:, c:c + 1], scalar2=None,
                                                                         op0=ALU.mult), reads=[gB, CB["ident"]], writes=[DB[di]])
                        P.op("dve", lambda e, c=c, di=di: e.tensor_scalar(out=Dl[di][:], in0=ident[:], scalar1=gl32[:, c:c + 1], scalar2=None,
                                                                         op0=ALU.mult), reads=[gB, CB["ident"]], writes=[DB[di]])
                        mm_group(ps[b][:, cc * 128:(cc + 1) * 128], [(ones_bf[:], Dh[di][:]), (ones_bf[:], Dl[di][:])],
                                 reads=[DB[di], CB["ones_bf"]], writes=[psB[b]])
                    P.op("act", lambda e, b=b, gi=gi, c4=c4: e.activation(out=gate_bc[gi][:, c4 * 512:(c4 + 1) * 512], in_=ps[b][:, :],
                                                                         func=AF.Identity), reads=[psB[b]], writes=[gateB[gi]])

            x1 = av(32, 64, F32).rearrange("p (b d) -> p b d", b=8)
            x1B = [Buf("x1_%d" % i) for i in range(8)]
            h2_fm = av(0, 32).rearrange("p (k t) -> p k t", k=16)
            h2Bs = [Buf("h2_t0"), Buf("h2_t1")]
            t4 = [av(144, 2, F32), av(146, 2, F32)]
            t4B = [Buf("t4a"), Buf("t4b")]
            hns4 = [(av(148, 4), Buf("hn4a")), (av(152, 4), Buf("hn4b"))]
            P.new_sem("ld_x1")
            for tb in range(8):
                P.op("sp", lambda e, tb=tb: e.dma_start(out=x1[:, tb, :], in_=xo[tb * 128:(tb + 1) * 128, :]),
                     writes=[x1B[tb]], sem="ld_x1", inc=16)
            fin = ("ld_x1", P.cnt["ld_x1"])
            for tb in range(8):
                x1B[tb].set_w(fin)

            def resid_epilogue(b, gi, s, tb, tbuf, tbufB):
                P.op("dve", lambda e: e.tensor_tensor(out=tbuf, in0=ps[b][:, :], in1=gate_bc[gi][:, s * 512:(s + 1) * 512], op=ALU.mult),
                     reads=[psB[b], gateB[gi]], writes=[tbufB])
                P.op("pool", lambda e: e.tensor_tensor(out=x1[:, tb, s * 512:(s + 1) * 512], in0=x1[:, tb, s * 512:(s + 1) * 512],
                                                       in1=tbuf, op=ALU.add), reads=[tbufB, x1B[tb]], writes=[x1B[tb]])

            np4 = NormPipe()
            for s in range(4):
                sl, sB, slot = wnext()
                for tb in range(8):
                    b = nbank()
                    mm_group(ps[b][:, :], [(merged[:, k, tb * 128:(tb + 1) * 128], sl[:, k, :]) for k in range(16)],
                             reads=sB + [mergedB], writes=[psB[b]])
                    ti = tc[0] % 2
                    tc[0] += 1
                    resid_epilogue(b, 0, s, tb, t4[ti], t4B[ti])
                    if s == 3:
                        np4.push((x1[:, tb, :], [x1B[tb]], None, hns4, h2_fm[:, :, tb * 128:(tb + 1) * 128], h2Bs[tb // 4],
                                  A2, B2, modB, False))
                wrelease(slot)
            np4.flush()

            a_q = av(112, 32).rearrange("p (f t) -> p f t", f=16)
            aqB = Buf("a_q")
            r5 = [av(96, 2, F32), av(98, 2, F32)]
            r5B = [Buf("r5a"), Buf("r5b")]
            t5 = [av(100, 2, F32), av(102, 2, F32)]
            t5B = [Buf("t5a"), Buf("t5b")]
            for q in range(4):
                for s in range(4):
                    sl, sB, slot = wnext()
                    for th in range(2):
                        for c in range(4):
                            b = nbank()
                            mm_group(ps[b][:, :], [(sl[:, k, c * 128:(c + 1) * 128], h2_fm[:, k, th * 512:(th + 1) * 512])
                                                   for k in range(16)], reads=sB + [h2Bs[th]], writes=[psB[b]])
                            ri = tc[0] % 2
                            tc[0] += 1
                            P.op("act", lambda e, b=b, ri=ri: e.activation(out=r5[ri], in_=ps[b][:, :], func=AF.Relu),
                                 reads=[psB[b]], writes=[r5B[ri]])
                            P.op("pool", lambda e, ri=ri, s=s, c=c, th=th: e.tensor_tensor(out=a_q[:, 4 * s + c, th * 512:(th + 1) * 512],
                                                                                          in0=r5[ri], in1=r5[ri], op=ALU.mult),
                                 reads=[r5B[ri]], writes=[aqB])
                    wrelease(slot)
                for s in range(4):
                    sl, sB, slot = wnext()
                    for tb in range(8):
                        b = nbank()
                        mm_group(ps[b][:, :], [(a_q[:, k, tb * 128:(tb + 1) * 128], sl[:, k, :]) for k in range(16)],
                                 reads=sB + [aqB], writes=[psB[b]])
                        ti = tc[0] % 2
                        tc[0] += 1
                        resid_epilogue(b, 1, s, tb, t5[ti], t5B[ti])
                    wrelease(slot)
        except _Stop:
            truncated = True
        P.new_sem("st_out")
        if truncated:
            x1 = av(32, 64, F32).rearrange("p (b d) -> p b d", b=8)
            x1B = [Buf("x1t_%d" % i) for i in range(8)]
            P.barrier()
            for tb in range(8):
                P.op("dve", lambda e, tb=tb: e.memset(x1[:, tb, :], 0.0), writes=[x1B[tb]])
        for tb in range(8):
            P.op("sp", lambda e, tb=tb: e.dma_start(out=out[tb * 128:(tb + 1) * 128, :], in_=x1[:, tb, :]),
                 reads=[x1B[tb]], sem="st_out", inc=16)
        P.wait_all("sp", ["st_out"] + [k for k in P.semnames if k.startswith("ld_")])
        if not truncated:
            assert wstate["next_use"] == len(sched), (wstate["next_use"], len(sched))
        print("op counts", {k: v for k, v in P.cnt.items()})
        P.build()
    return nc


def _consts():
    ident = np.eye(128, dtype=np.float32)
    j = np.arange(128)
    negtri = -(j[:, None] >= j[None, :]).astype(np.float32)
    ones = np.ones((128, 128), np.float32)
    E = np.zeros((128, 16, 16), np.float32)
    for p in range(16):
        E[:, p, p] = 1.0
    S = np.zeros((16, 16, 128), np.float32)
    for p in range(16):
        S[p + 1:, p, :] = -1.0
    return ident, negtri, ones, E.reshape(128, 256), S.reshape(16, 2048)


def _negmask(half):
    goff = [0, 3] if half == 0 else [1, 2]
    m = np.zeros((128, 4, 256), np.float32)
    s = np.arange(128)[:, None]
    t = np.arange(128)[None, :]
    tri = np.where(s < t, 0.0, NEG).astype(np.float32)
    for o in range(4):
        for qh in range(2):
            sl = slice(qh * 128, (qh + 1) * 128)
            if o < goff[qh]:
                m[:, o, sl] = 0.0
            elif o == goff[qh]:
                m[:, o, sl] = tri
            else:
                m[:, o, sl] = NEG
    return m.reshape(128, 1024)


_NC_CACHE = {}


def kernel(x, c, w_ada, b_ada, norm1_w, w_in, q_norm_w, k_norm_w, w_pool, pool_scale,
           w_a_up, w_b_up, w_o, norm2_w, w_ff1, w_ff2):
    f = lambda a: np.ascontiguousarray(np.asarray(a, dtype=np.float32))
    x = f(x); c = f(c)
    if "nc" not in _NC_CACHE:
        _NC_CACHE["nc"] = build_program()
    nc = _NC_CACHE["nc"]
    ident, negtri, ones, E, S = _consts()
    bada_fm = f(np.asarray(b_ada)[0].reshape(96, 128).T)
    n1 = f(np.asarray(norm1_w)[0].reshape(16, 128).T)
    n2 = f(np.asarray(norm2_w)[0].reshape(16, 128).T)
    qkw = f(np.stack([np.asarray(q_norm_w)[0], np.asarray(k_norm_w)[0]], axis=1))
    psc = f(np.asarray(pool_scale)[0].reshape(8, 128).T)
    shared = {
        "bada": bada_fm, "n1w": n1, "n2w": n2, "qkw": qkw, "pscale": psc,
        "c_ident": ident, "c_negtri": negtri, "c_ones": ones, "c_E": E, "c_S": S,
        "w_ada": f(np.asarray(w_ada)[0]), "w_in": f(np.asarray(w_in)[0]), "w_pool": f(np.asarray(w_pool)[0]),
        "w_a_up": f(np.asarray(w_a_up)[0]), "w_b_up": f(np.asarray(w_b_up)[0]), "w_o": f(np.asarray(w_o)[0]),
        "w_ff1": f(np.asarray(w_ff1)[0]), "w_ff2": f(np.asarray(w_ff2)[0]),
    }
    in_maps = []
    for r in range(8):
        b, half = r // 2, r % 2
        own = OWN[half]
        xb = x[b]
        xo = np.zeros((1152, 2048), np.float32)
        hv = np.zeros((128, 8, 16), np.float32)
        for j, g in enumerate(own):
            xo[j * 128:(j + 1) * 128] = xb[g * 128:(g + 1) * 128]
            if g > 0:
                xo[1024 + j * 16:1024 + (j + 1) * 16] = xb[g * 128 - 16:g * 128]
                hv[:, j, :] = 1.0
        invc = np.zeros((128, 4, 128), np.float32)
        for g, w in enumerate([2, 4, 8, 16]):
            if half == 0:
                cnt = np.minimum(np.arange(128) + 1, w).astype(np.float32)
            else:
                cnt = np.full(128, w, np.float32)
            invc[:, g, :] = (1.0 / cnt)[None, :]
        m = dict(shared)
        m.update({
            "xa": xb, "xo": xo, "csil": f(c[b].reshape(16, 128).T), "hv16": hv.reshape(128, 128),
            "invc": invc.reshape(128, 512), "c_negmask": _negmask(half),
        })
        in_maps.append(m)
    res = run_bass_kernel_spmd(nc, in_maps, core_ids=list(range(8)))
    outp = np.empty((4, 2048, 2048), np.float32)
    for r in range(8):
        b, half = r // 2, r % 2
        o = res.results[r]["out"]
        for j, g in enumerate(OWN[half]):
            outp[b, g * 128:(g + 1) * 128] = o[j * 128:(j + 1) * 128]
    return outp
```

```python
from contextlib import ExitStack

import numpy as np
import concourse.bass as bass
import concourse.mybir as mybir
from concourse.bass_utils import run_bass_kernel_spmd

F32 = mybir.dt.float32
BF16 = mybir.dt.bfloat16
AF = mybir.ActivationFunctionType
ALU = mybir.AluOpType

import os
STAGE = int(os.environ.get("KSTAGE", "99"))
SUB = int(os.environ.get("KSUB", "99"))
VAR = int(os.environ.get("KVAR", "0"))


class _Stop(Exception):
    pass


ENGS = ["pe", "act", "dve", "pool", "sp"]
NEG = -30000.0
EPS = 1e-6
OWN = [[0, 3, 4, 7, 8, 11, 12, 15], [1, 2, 5, 6, 9, 10, 13, 14]]


class Buf:
    __slots__ = ("name", "w", "r")

    def __init__(self, name):
        self.name = name
        self.w = {}
        self.r = {}

    def set_w(self, ev):
        self.w = {ev[0]: ev[1]}


class Prog:
    def __init__(self, nc):
        self.nc = nc
        self.streams = {e: [] for e in ENGS}
        self.cnt = {}
        self.waited = {}
        self.sems = {}
        self.semnames = []
        for e in ENGS:
            self.new_sem("E_" + e)

    def new_sem(self, key):
        self.semnames.append(key)
        self.cnt[key] = 0
        return key

    def op(self, eng, fn, reads=(), writes=(), sem=None, inc=1, pwrites=()):
        need = {}

        def add(k, v):
            if need.get(k, 0) < v:
                need[k] = v
        for b in reads:
            for k, v in b.w.items():
                add(k, v)
        for b in writes:
            for k, v in b.w.items():
                add(k, v)
            for k, v in b.r.items():
                add(k, v)
        for b in pwrites:
            for k, v in b.r.items():
                add(k, v)
        waits = []
        for k, v in need.items():
            if self.waited.get((eng, k), 0) < v:
                self.waited[(eng, k)] = v
                waits.append((k, v))
        key = sem if sem is not None else "E_" + eng
        self.cnt[key] += inc
        ev = (key, self.cnt[key])
        sems = self.sems

        def run(e, waits=waits, fn=fn, key=key, inc=inc):
            for k, v in waits:
                e.wait_ge(sems[k], v)
            ins = fn(e)
            ins.then_inc(sems[key], inc)
        self.streams[eng].append(run)
        for b in writes:
            b.w = {ev[0]: ev[1]}
            b.r = {}
        for b in pwrites:
            if b.w.get(ev[0], 0) < ev[1]:
                b.w[ev[0]] = ev[1]
        for b in reads:
            if b.r.get(ev[0], 0) < ev[1]:
                b.r[ev[0]] = ev[1]
        return ev

    def barrier(self):
        snap = {("E_" + e): self.cnt["E_" + e] for e in ENGS}
        sems = self.sems
        for e in ENGS:
            waits = []
            for k, v in snap.items():
                if v > 0 and self.waited.get((e, k), 0) < v:
                    self.waited[(e, k)] = v
                    waits.append((k, v))
            if waits:
                def run(h, waits=waits):
                    for k, v in waits:
                        h.wait_ge(sems[k], v)
                self.streams[e].append(run)

    def wait_all(self, eng, keys):
        sems = self.sems
        waits = [(k, self.cnt[k]) for k in keys if self.cnt[k] > 0]

        def run(h, waits=waits):
            for k, v in waits:
                h.wait_ge(sems[k], v)
        self.streams[eng].append(run)

    def build(self):
        nc = self.nc
        with ExitStack() as st:
            for k in self.semnames:
                self.sems[k] = st.enter_context(nc.semaphore(k))
            block = st.enter_context(nc.Block())

            @block.tensor
            def _(h):
                for f in self.streams["pe"]:
                    f(h)

            @block.scalar
            def _(h):
                for f in self.streams["act"]:
                    f(h)

            @block.vector
            def _(h):
                for f in self.streams["dve"]:
                    f(h)

            @block.gpsimd
            def _(h):
                for f in self.streams["pool"]:
                    f(h)

            @block.sync
            def _(h):
                for f in self.streams["sp"]:
                    f(h)


def build_program():
    nc = bass.Bass("TRN2", target_bir_lowering=False)
    P = Prog(nc)

    def din(name, shape):
        return nc.dram_tensor(name, list(shape), F32, kind="ExternalInput").ap()

    xa = din("xa", [2048, 2048])
    xo = din("xo", [1152, 2048])
    csil = din("csil", [128, 16])
    bada = din("bada", [128, 96])
    n1w = din("n1w", [128, 16])
    n2w = din("n2w", [128, 16])
    qkw = din("qkw", [128, 2])
    pscale = din("pscale", [128, 8])
    hv16 = din("hv16", [128, 128])
    invc = din("invc", [128, 512])
    c_ident = din("c_ident", [128, 128])
    c_negtri = din("c_negtri", [128, 128])
    c_ones = din("c_ones", [128, 128])
    c_E = din("c_E", [128, 256])
    c_S = din("c_S", [16, 2048])
    c_negmask = din("c_negmask", [128, 1024])
    w_ada = din("w_ada", [2048, 12288])
    w_in = din("w_in", [2048, 8192])
    w_pool = din("w_pool", [4, 256, 256])
    w_a_up = din("w_a_up", [1024, 2048])
    w_b_up = din("w_b_up", [1024, 2048])
    w_o = din("w_o", [2048, 2048])
    w_ff1 = din("w_ff1", [2048, 8192])
    w_ff2 = din("w_ff2", [8192, 2048])
    out = nc.dram_tensor("out", [1024, 2048], F32, kind="ExternalOutput").ap()

    with ExitStack() as st:
        def sb(name, shape, dt):
            return st.enter_context(nc.sbuf_tensor(name, list(shape), dt))

        ARENA_KB = 159
        arena = sb("arena", [128, ARENA_KB * 512], BF16)

        def av(kb_off, kb_len, dt=BF16):
            a = arena[:, int(kb_off * 512):int((kb_off + kb_len) * 512)]
            return a.bitcast(F32) if dt == F32 else a

        slab_t = [sb("slab0", [128, 16, 512], BF16), sb("slab1", [128, 16, 512], BF16)]
        ident = sb("ident", [128, 128], BF16)
        negtri = sb("negtri", [128, 128], BF16)
        ones_bf = sb("ones_bf", [128, 128], BF16)
        negones = sb("negones", [128, 128], BF16)
        Et = sb("Et", [128, 16, 16], BF16)
        St = sb("St", [16, 16, 128], BF16)
        negmask = sb("negmask", [128, 4, 256], BF16)
        invct = sb("invct", [128, 4, 128], F32)
        hvt = sb("hvt", [128, 8, 16], F32)
        wpool = sb("wpool", [128, 4, 2, 256], BF16)
        csil_t = sb("csil_t", [128, 16], F32)
        sig_t = sb("sig_t", [128, 16], F32)
        scT = sb("scT", [128, 16], BF16)
        bada_t = sb("bada_t", [128, 96], F32)
        modfm = sb("modfm", [128, 96], F32)
        n1w_t = sb("n1w_t", [128, 16], F32)
        n2w_t = sb("n2w_t", [128, 16], F32)
        A1 = sb("A1", [128, 16], F32)
        A2 = sb("A2", [128, 16], F32)
        qkw_t = sb("qkw_t", [128, 2], F32)
        qw_s = sb("qw_s", [128, 1], F32)
        psc_t = sb("psc_t", [128, 8], F32)
        gh_bf = sb("gh_bf", [128, 16], BF16)
        gl_bf = sb("gl_bf", [128, 16], BF16)
        gT = sb("gT", [16, 256], BF16)
        gh32 = sb("gh32", [128, 16], F32)
        gl32 = sb("gl32", [128, 16], F32)
        sst = [sb("ss0", [128, 4], F32), sb("ss1", [128, 4], F32)]
        ps = [st.enter_context(nc.psum_tensor("ps%d" % i, [128, 512], F32)) for i in range(8)]
        psB = [Buf("ps%d" % i) for i in range(8)]

        P.new_sem("ld_c_pool")
        P.new_sem("ld_c_sp")
        cbufs = {}
        cq = {}

        def cload(eng, dst, src, name):
            b = Buf(name)
            cbufs[name] = b
            cq[name] = eng
            P.op(eng, lambda e: e.dma_start(out=dst, in_=src), writes=[b], sem="ld_c_" + eng, inc=16)
            return b

        cload("pool", ident[:], c_ident, "ident")
        cload("pool", negtri[:], c_negtri, "negtri")
        cload("pool", ones_bf[:], c_ones, "ones_bf")
        cload("pool", Et[:], c_E.rearrange("p (a b) -> p a b", a=16), "E")
        cload("pool", St[:], c_S.rearrange("p (a b) -> p a b", a=16), "S")
        cload("pool", negmask[:], c_negmask.rearrange("p (a b) -> p a b", a=4), "negmask")
        cload("pool", wpool[:], w_pool.rearrange("g (c p) e -> p g c e", p=128), "wpool")
        cload("sp", invct[:], invc.rearrange("p (a b) -> p a b", a=4), "invc")
        cload("sp", hvt[:], hv16.rearrange("p (a b) -> p a b", a=8), "hv")
        cload("sp", csil_t[:], csil, "csil")
        cload("sp", bada_t[:], bada, "bada")
        cload("sp", n1w_t[:], n1w, "n1w")
        cload("sp", n2w_t[:], n2w, "n2w")
        cload("sp", qkw_t[:], qkw, "qkw")
        cload("sp", psc_t[:], pscale, "psc")
        for name, b in cbufs.items():
            b.set_w(("ld_c_" + cq[name], P.cnt["ld_c_" + cq[name]]))
        CB = cbufs
        CB["negones"] = Buf("negones")

        for i in range(2):
            P.new_sem("ld_s%dlo" % i)
            P.new_sem("ld_s%dhi" % i)
        slabB = [(Buf("s0lo"), Buf("s0hi")), (Buf("s1lo"), Buf("s1hi"))]

        def rows16(w, r0, c0, nk=16):
            return w[r0:r0 + nk * 128, c0:c0 + 512].rearrange("(k p) c -> p k c", p=128)

        sched = []
        for n in list(range(0, 8)):
            sched.append([("full", rows16(w_ada, 0, n * 512))])
        for s in [0, 1, 2, 3]:
            sched.append([("full", rows16(w_in, 0, s * 512))])
        for _half in range(2):
            for s in [4, 5, 6, 7]:
                sched.append([("full", rows16(w_in, 0, s * 512))])
        ADA_LATE = list(range(12, 20)) + [8, 9, 10, 11, 20, 21, 22, 23]
        for n in ADA_LATE:
            sched.append([("full", rows16(w_ada, 0, n * 512))])
        for s in range(4):
            sched.append([("full", rows16(w_in, 0, 4096 + s * 512))])
            sched.append([("lo", rows16(w_a_up, 0, s * 512, 8)), ("hi", rows16(w_b_up, 0, s * 512, 8))])
            sched.append([("full", rows16(w_in, 0, 6144 + s * 512))])
        for s in range(4):
            sched.append([("full", rows16(w_o, 0, s * 512))])
        for q in range(4):
            for s in range(4):
                sched.append([("full", rows16(w_ff1, 0, q * 2048 + s * 512))])
            for s in range(4):
                sched.append([("full", rows16(w_ff2, q * 2048, s * 512))])

        wstate = {"next_issue": 0, "next_use": 0, "slot_of": {}}
        free_slots = [0, 1]

        def issue_next():
            while free_slots and wstate["next_issue"] < len(sched):
                i = wstate["next_issue"]
                slot = free_slots.pop(0)
                wstate["slot_of"][i] = slot
                lo, hi = slabB[slot]
                for (half, src) in sched[i]:
                    if half == "full":
                        P.op("pool", lambda e, slot=slot, src=src: e.dma_start(out=slab_t[slot][:], in_=src),
                             writes=[lo, hi], sem="ld_s%dlo" % slot, inc=16)
                    elif half == "lo":
                        P.op("pool", lambda e, slot=slot, src=src: e.dma_start(out=slab_t[slot][:, 0:8, :], in_=src),
                             writes=[lo], sem="ld_s%dlo" % slot, inc=16)
                    else:
                        P.op("pool", lambda e, slot=slot, src=src: e.dma_start(out=slab_t[slot][:, 8:16, :], in_=src),
                             writes=[hi], sem="ld_s%dhi" % slot, inc=16)
                wstate["next_issue"] += 1

        def wnext():
            i = wstate["next_use"]
            wstate["next_use"] += 1
            if i not in wstate["slot_of"]:
                issue_next()
            slot = wstate["slot_of"][i]
            return slab_t[slot], list(slabB[slot]), slot

        def wrelease(slot):
            free_slots.append(slot)
            issue_next()

        issue_next()

        def mm_group(out_ap, pairs, reads, writes):
            n = len(pairs)

            def fn(e):
                ins = None
                for i, (l, r) in enumerate(pairs):
                    ins = e.matmul(out_ap, lhsT=l, rhs=r, start=(i == 0), stop=(i == n - 1))
                return ins
            return P.op("pe", fn, reads=reads, writes=writes)

        nrm_ctr = [0]

        def norm_block(src_ap, src_bufs, xt_bufs, hns, dst, dstB, A_t, B_t, modB, from_dram):
            i = nrm_ctr[0]
            nrm_ctr[0] += 1
            ss = sst[i % 2]
            ssB = ssBs[i % 2]
            hn_t, hnB = hns[i % len(hns)]
            if from_dram:
                xt_t, xtB, ldk = xt_bufs[i % len(xt_bufs)]
                P.op("sp", lambda e: e.dma_start(out=xt_t, in_=src_ap), writes=[xtB], sem=ldk, inc=16)
                xin, xinB = xt_t, [xtB]
            else:
                xin, xinB = src_ap, src_bufs
            P.op("dve", lambda e: e.memset(ss[:, 0:1], 0.0), writes=[ssB])
            P.op("act", lambda e: e.activation(out=hn_t, in_=xin, func=AF.Square, accum_out=ss[:, 0:1]),
                 reads=xinB, writes=[hnB, ssB])
            P.op("act", lambda e: e.activation(out=ss[:, 1:2], in_=ss[:, 0:1], func=AF.Ln, scale=1.0 / 2048, bias=EPS),
                 reads=[ssB], writes=[ssB])
            P.op("act", lambda e: e.activation(out=ss[:, 2:3], in_=ss[:, 1:2], func=AF.Exp, scale=-0.5), reads=[ssB], writes=[ssB])
            P.op("dve", lambda e: e.tensor_scalar(out=hn_t, in0=xin, scalar1=ss[:, 2:3], scalar2=None, op0=ALU.mult),
                 reads=xinB + [ssB], writes=[hnB])
            b0, b1 = [(4, 5), (6, 7)][i % 2]
            pv = [ps[b0][:].bitcast(BF16), ps[b1][:].bitcast(BF16)]

            def back():
                def tr(e):
                    ins = None
                    for c in range(16):
                        ins = e.transpose(pv[c // 8][:, (c % 8) * 128:(c % 8 + 1) * 128], hn_t[:, c * 128:(c + 1) * 128], ident[:])
                    return ins
                P.op("pe", tr, reads=[hnB, CB["ident"]], writes=[psB[b0], psB[b1]])
                for c in range(16):
                    src = pv[c // 8][:, (c % 8) * 128:(c % 8 + 1) * 128]
                    if c < 8:
                        P.op("act", lambda e, c=c, src=src: e.activation(out=dst[:, c, :], in_=src, func=AF.Identity,
                                                                         bias=B_t[:, c:c + 1], scale=A_t[:, c:c + 1]),
                             reads=[psB[b0], modB], pwrites=[dstB])
                    else:
                        P.op("dve", lambda e, c=c, src=src: e.tensor_scalar(out=dst[:, c, :], in0=src, scalar1=A_t[:, c:c + 1],
                                                                            scalar2=B_t[:, c:c + 1], op0=ALU.mult, op1=ALU.add),
                             reads=[psB[b1], modB], pwrites=[dstB])
            return back

        class NormPipe:
            def __init__(self):
                self.pending = None

            def push(self, args):
                bk = norm_block(*args)
                if self.pending is not None:
                    self.pending()
                self.pending = bk

            def flush(self):
                if self.pending is not None:
                    self.pending()
                self.pending = None

        def norm_seq(arglist):
            npipe = NormPipe()
            for args in arglist:
                npipe.push(args)
            npipe.flush()

        ssBs = [Buf("ss0"), Buf("ss1")]
        qk_ctr = [0]

        def qknorm(psv, psbuf, wcol, wB, dest, destB, tset):
            sq, raw, rt, rinv, tB = tset
            sb_i = 3
            LV = VAR if VAR >= 10 else 99
            if VAR == 15:
                P.op("act", lambda e: e.activation(out=sq, in_=psv, func=AF.Square), reads=[psbuf], writes=[tB["sq"]])
                P.op("dve", lambda e: e.tensor_copy(out=dest, in_=psv), reads=[psbuf], writes=[destB])
                return
            if VAR == 16:
                P.op("dve", lambda e: e.tensor_copy(out=raw, in_=psv), reads=[psbuf], writes=[tB["raw"]])
                P.op("dve", lambda e: e.tensor_copy(out=dest, in_=raw), reads=[tB["raw"]], writes=[destB])
                return
            P.op("dve", lambda e: e.tensor_copy(out=raw, in_=psv), reads=[psbuf], writes=[tB["raw"]])
            P.op("act", lambda e: e.activation(out=sq, in_=raw, func=AF.Square), reads=[tB["raw"]], writes=[tB["sq"]])
            if LV >= 12:
                mm_group(ps[sb_i][:, :], [(ones_bf[:], sq)], reads=[tB["sq"], CB["ones_bf"]], writes=[psB[sb_i]])
                P.op("act", lambda e: e.activation(out=rt, in_=ps[sb_i][:, :], func=AF.Ln, scale=1.0 / 128, bias=EPS),
                     reads=[psB[sb_i]], writes=[tB["rt"]])
            if LV >= 13:
                P.op("act", lambda e: e.activation(out=rinv, in_=rt, func=AF.Exp, scale=-0.5), reads=[tB["rt"]], writes=[tB["rinv"]])
            if LV >= 14:
                P.op("dve", lambda e: e.scalar_tensor_tensor(out=dest, in0=raw, scalar=wcol, in1=rinv, op0=ALU.mult, op1=ALU.mult),
                     reads=[tB["raw"], tB["rinv"], wB], writes=[destB])
            else:
                P.op("dve", lambda e: e.tensor_copy(out=dest, in_=raw), reads=[tB["raw"]], writes=[destB])

        def mk_tset(kb):
            sq = av(kb, 1)
            raw = av(kb + 1, 2, F32)
            rt = av(kb + 3, 2, F32)
            rinv = av(kb + 5, 2, F32)
            return (sq, raw, rt, rinv, {k: Buf(k) for k in ["sq", "raw", "rt", "rinv"]})

        truncated = False
        try:
            modB = Buf("modfm")
            P.op("act", lambda e: e.activation(out=sig_t[:], in_=csil_t[:], func=AF.Sigmoid), reads=[CB["csil"]], writes=[modB])
            scB = Buf("scT")
            P.op("dve", lambda e: e.tensor_tensor(out=scT[:], in0=sig_t[:], in1=csil_t[:], op=ALU.mult),
                 reads=[modB, CB["csil"]], writes=[scB])
            P.op("dve", lambda e: e.tensor_scalar(out=qw_s[:], in0=qkw_t[:, 0:1], scalar1=float(128.0 ** -0.5), scalar2=None,
                                                  op0=ALU.mult), reads=[CB["qkw"]], writes=[CB["qkw"]])
            ada_ctr = [0]

            def ada_cols(n, bank=None):
                j = ada_ctr[0] % 2 if bank is None else bank
                ada_ctr[0] += 1
                sl, sB, slot = wnext()

                def fn(e, sl=sl, j=j):
                    ins = None
                    for c in range(4):
                        for k in range(16):
                            ins = e.matmul(ps[j][:, c:c + 1], lhsT=sl[:, k, c * 128:(c + 1) * 128], rhs=scT[:, k:k + 1],
                                           start=(k == 0), stop=(k == 15))
                    return ins
                P.op("pe", fn, reads=sB + [scB], writes=[psB[j]])
                wrelease(slot)
                P.op("dve", lambda e, n=n, j=j: e.tensor_tensor(out=modfm[:, 4 * n:4 * n + 4], in0=ps[j][:, 0:4],
                                                                in1=bada_t[:, 4 * n:4 * n + 4], op=ALU.add),
                     reads=[psB[j], CB["bada"]], writes=[modB])

            for n in list(range(0, 8)):
                ada_cols(n)
            P.op("dve", lambda e: e.scalar_tensor_tensor(out=A1[:], in0=modfm[:, 16:32], scalar=1.0, in1=n1w_t[:],
                                                         op0=ALU.add, op1=ALU.mult), reads=[modB, CB["n1w"]], writes=[modB])
            B1 = modfm[:, 0:16]
            B2 = modfm[:, 48:64]

            P.barrier()
            if STAGE == 1:
                raise _Stop()
            q_fm = av(0, 16).rearrange("p (h t) -> p h t", h=8)
            mixed = av(16, 16).rearrange("p (c t) -> p c t", c=8)
            h_own = av(32, 36).rearrange("p (k t) -> p k t", k=16)
            xtb = []
            for i in range(2):
                k = P.new_sem("ld_xt%d" % i)
                xtb.append((av(68 + 8 * i, 8, F32), Buf("xt%d" % i), k))
            hns1 = [(av(84, 4), Buf("hn_a")), (av(120, 4), Buf("hn_b"))]
            tsets = [mk_tset(88), mk_tset(95)]
            ucat = av(102, 4.5, F32).rearrange("p (b t) -> p b t", b=8)
            tmpA = av(106.5, 4.5, F32).rearrange("p (b t) -> p b t", b=8)
            tmpB = av(111, 4.5, F32).rearrange("p (b t) -> p b t", b=8)
            pooled = av(115.5, 4).rearrange("p (c t) -> p c t", c=2)
            tmp0 = av(119.5, 0.5, F32)
            qB, mixB = Buf("q"), Buf("mixed")
            hownBs = [Buf("h_own_t0"), Buf("h_own_t1"), Buf("h_own_halo")]
            ucatB, tmpAB, tmpBB, pooledB, tmp0B = Buf("ucat"), Buf("tmpA"), Buf("tmpB"), Buf("pooled"), Buf("tmp0")

            norm_seq([(xo[tb * 128:(tb + 1) * 128, :], None, xtb, hns1,
                       h_own[:, :, tb * 128:(tb + 1) * 128], hownBs[tb // 4], A1, B1, modB, True) for tb in range(9)])

            if STAGE == 2 and SUB == 1:
                P.barrier()
                raise _Stop()
            pbank = [0]

            def nbank(lo=0, hi=3):
                b = lo + pbank[0] % (hi - lo)
                pbank[0] += 1
                return b

            def mk_uset(kb, tag):
                return {"ucat": av(kb, 4.5, F32).rearrange("p (b t) -> p b t", b=8),
                        "tmpA": av(kb + 4.5, 4.5, F32).rearrange("p (b t) -> p b t", b=8),
                        "tmpB": av(kb + 9, 4.5, F32).rearrange("p (b t) -> p b t", b=8),
                        "tmp0": av(kb + 13.5, 0.5, F32),
                        "ucatB": Buf("ucat" + tag), "tmpAB": Buf("tmpA" + tag), "tmpBB": Buf("tmpB" + tag), "tmp0B": Buf("tmp0" + tag)}
            usets = [{"ucat": ucat, "tmpA": tmpA, "tmpB": tmpB, "tmp0": tmp0,
                      "ucatB": ucatB, "tmpAB": tmpAB, "tmpBB": tmpBB, "tmp0B": tmp0B}, mk_uset(124, "2")]
            pooledS = [(pooled, pooledB), (av(138, 4).rearrange("p (c t) -> p c t", c=2), Buf("pooled2"))]
            pending_mix = [None]

            def emit_mix(g):
                pl, plB = pooledS[g % 2]
                for ec in range(2):
                    for th in range(2):
                        b = nbank()
                        mm_group(ps[b][:, :], [(wpool[:, g, k, ec * 128:(ec + 1) * 128], pl[:, k, th * 512:(th + 1) * 512])
                                               for k in range(2)], reads=[CB["wpool"], plB], writes=[psB[b]])
                        P.op("act", lambda e, b=b, g=g, ec=ec, th=th: e.activation(
                            out=mixed[:, 2 * g + ec, th * 512:(th + 1) * 512], in_=ps[b][:, :], func=AF.Identity,
                            scale=psc_t[:, 2 * g + ec:2 * g + ec + 1]), reads=[psB[b], CB["psc"]], writes=[mixB])

            for s in range(2):
                sl, sB, slot = wnext()
                for uc4 in range(4):
                    uc = s * 4 + uc4
                    g = uc // 2
                    cc = uc % 2
                    w = 2 ** (g + 1)
                    U = usets[uc % 2]
                    uct, uctB = U["ucat"], U["ucatB"]
                    pl, plB = pooledS[g % 2]
                    bm = []
                    for th in range(2):
                        b = nbank()
                        mm_group(ps[b][:, :], [(sl[:, k, uc4 * 128:(uc4 + 1) * 128], h_own[:, k, th * 512:(th + 1) * 512])
                                               for k in range(16)], reads=sB + [hownBs[th]], writes=[psB[b]])
                        bm.append(b)
                    bh = nbank()
                    mm_group(ps[bh][:, 0:128], [(sl[:, k, uc4 * 128:(uc4 + 1) * 128], h_own[:, k, 1024:1152]) for k in range(16)],
                             reads=sB + [hownBs[2]], writes=[psB[bh]])
                    P.op("act", lambda e, b=bm[0], uct=uct: e.activation(out=uct[:, 0:4, 16:144],
                                                                         in_=ps[b][:, :].rearrange("p (b t) -> p b t", b=4), func=AF.Identity),
                         reads=[psB[bm[0]]], pwrites=[uctB])
                    P.op("act", lambda e, b=bm[1], uct=uct: e.activation(out=uct[:, 4:8, 16:144],
                                                                         in_=ps[b][:, :].rearrange("p (b t) -> p b t", b=4), func=AF.Identity),
                         reads=[psB[bm[1]]], pwrites=[uctB])
                    P.op("dve", lambda e, b=bh, uct=uct: e.tensor_tensor(out=uct[:, :, 0:16], in0=ps[b][:, 0:128].rearrange("p (b t) -> p b t", b=8),
                                                                         in1=hvt[:], op=ALU.mult), reads=[psB[bh], CB["hv"]], pwrites=[uctB])
                    if pending_mix[0] is not None:
                        emit_mix(pending_mix[0])
                        pending_mix[0] = None
                    tA, tAB, tB_, tBB = U["tmpA"], U["tmpAB"], U["tmpB"], U["tmpBB"]
                    P.op("dve", lambda e, tA=tA, uct=uct: e.tensor_tensor(out=tA[:, :, 1:144], in0=uct[:, :, 1:144], in1=uct[:, :, 0:143], op=ALU.add),
                         reads=[uctB], writes=[tAB])
                    cur, curB, oth, othB = tA, tAB, tB_, tBB
                    sh = 2
                    while sh < w:
                        lo_ = 2 * sh - 1
                        P.op("dve", lambda e, cur=cur, oth=oth, lo_=lo_, sh=sh: e.tensor_tensor(
                            out=oth[:, :, lo_:144], in0=cur[:, :, lo_:144], in1=cur[:, :, lo_ - sh:144 - sh], op=ALU.add),
                            reads=[curB], writes=[othB])
                        cur, curB, oth, othB = oth, othB, cur, curB
                        sh *= 2
                    t0_, t0B_ = U["tmp0"], U["tmp0B"]
                    P.op("dve", lambda e, cur=cur, w=w, cc=cc, pl=pl, uct=uct: e.scalar_tensor_tensor(
                        out=pl[:, cc, 128:1024].rearrange("p (b t) -> p b t", b=7), in0=cur[:, 1:8, 16:144], scalar=1.0 / w,
                        in1=uct[:, 1:8, 16:144], op0=ALU.mult, op1=ALU.subtract), reads=[curB, uctB], pwrites=[plB])
                    P.op("dve", lambda e, cur=cur, g=g, t0_=t0_: e.tensor_tensor(out=t0_, in0=cur[:, 0, 16:144], in1=invct[:, g, :], op=ALU.mult),
                         reads=[curB, CB["invc"]], writes=[t0B_])
                    P.op("dve", lambda e, cc=cc, pl=pl, t0_=t0_, uct=uct: e.tensor_tensor(out=pl[:, cc, 0:128], in0=t0_, in1=uct[:, 0, 16:144],
                                                                                          op=ALU.subtract),
                         reads=[t0B_, uctB], pwrites=[plB])
                    if cc == 1:
                        pending_mix[0] = g
                wrelease(slot)
            if pending_mix[0] is not None:
                emit_mix(pending_mix[0])
                pending_mix[0] = None
            if STAGE == 2 and SUB == 2:
                P.barrier()
                raise _Stop()
            qc = [0]
            pend = None
            for s in range(2):
                sl, sB, slot = wnext()
                for h4 in range(4):
                    h = s * 4 + h4
                    for th in range(2):
                        b = nbank()
                        mm_group(ps[b][:, :], [(sl[:, k, h4 * 128:(h4 + 1) * 128], h_own[:, k, th * 512:(th + 1) * 512])
                                               for k in range(16)], reads=sB + [hownBs[th]], writes=[psB[b]])
                        if pend is not None:
                            qknorm(*pend)
                        pend = (ps[b][:, :], psB[b], qw_s[:, 0:1], CB["qkw"], q_fm[:, h, th * 512:(th + 1) * 512], qB,
                                tsets[qc[0] % 2])
                        qc[0] += 1
                wrelease(slot)
            qknorm(*pend)

            P.barrier()
            if STAGE == 2:
                raise _Stop()
            k_fm = av(32, 32).rearrange("p (h t) -> p h t", h=8)
            v_tm = av(64, 32).rearrange("p (b c) -> p b c", b=16)
            h_half = av(96, 32).rearrange("p (k t) -> p k t", k=16)
            kx = P.new_sem("ld_xta")
            xta = [(av(128, 8, F32), Buf("xta"), kx)]
            hns2 = [(av(136, 4), Buf("hn2a")), (av(140, 4), Buf("hn2b"))]
            tsets_a = [mk_tset(144), mk_tset(151)]
            kc = [0]
            kB, vB = Buf("k"), Buf("v")
            hhBs = [Buf("h_half_%d" % i) for i in range(8)]

            def nargs(half, tb):
                return (xa[(half * 8 + tb) * 128:(half * 8 + tb + 1) * 128, :], None, xta, hns2,
                        h_half[:, :, tb * 128:(tb + 1) * 128], hhBs[tb], A1, B1, modB, True)
            npipe = NormPipe()
            for tb in range(8):
                npipe.push(nargs(0, tb))
            npipe.flush()
            for half in range(2):
                pend = None
                for s in range(2):
                    sl, sB, slot = wnext()
                    for th in range(2):
                        for h4 in range(4):
                            h = s * 4 + h4
                            b = nbank()
                            mm_group(ps[b][:, :], [(sl[:, k, h4 * 128:(h4 + 1) * 128], h_half[:, k, th * 512:(th + 1) * 512])
                                                   for k in range(16)], reads=sB + hhBs[4 * th:4 * th + 4], writes=[psB[b]])
                            t0 = half * 1024 + th * 512
                            if pend is not None:
                                qknorm(*pend)
                            pend = (ps[b][:, :], psB[b], qkw_t[:, 1:2], CB["qkw"], k_fm[:, h, t0:t0 + 512], kB, tsets_a[kc[0] % 2])
                            kc[0] += 1
                    wrelease(slot)
                qknorm(*pend)
                for s in range(2):
                    sl, sB, slot = wnext()
                    for tb in range(8):
                        b = nbank()
                        mm_group(ps[b][:, :], [(h_half[:, k, tb * 128:(tb + 1) * 128], sl[:, k, :]) for k in range(16)],
                                 reads=sB + [hhBs[tb]], writes=[psB[b]])
                        gtb = half * 8 + tb
                        if tb % 2 == 0:
                            P.op("act", lambda e, b=b, gtb=gtb, s=s: e.activation(out=v_tm[:, gtb, s * 512:(s + 1) * 512], in_=ps[b][:, :],
                                                                                 func=AF.Identity), reads=[psB[b]], pwrites=[vB])
                        else:
                            P.op("dve", lambda e, b=b, gtb=gtb, s=s: e.tensor_copy(out=v_tm[:, gtb, s * 512:(s + 1) * 512], in_=ps[b][:, :]),
                                 reads=[psB[b]], pwrites=[vB])
                        if half == 0 and s == 1:
                            npipe.push(nargs(1, tb))
                    wrelease(slot)
                if half == 0:
                    npipe.flush()

            P.barrier()
            if STAGE == 3:
                raise _Stop()
            o_fm = av(96, 16).rearrange("p (h t) -> p h t", h=8)
            oB = Buf("o_fm")

            def ring(kb0, kb_each, n, dt, name):
                return [(av(kb0 + j * kb_each, kb_each, dt), Buf("%s%d" % (name, j))) for j in range(n)]
            e_r = ring(112, 2, 3, F32, "e")
            l_r = ring(118, 1, 3, BF16, "l")
            a_r = ring(121, 1, 3, BF16, "a")
            R32 = ring(124, 2, 2, F32, "R32")
            Rb = ring(128, 1, 2, BF16, "Rb")
            zeros_bf = av(130, 1)
            zerosB = Buf("zeros")
            P.op("dve", lambda e: e.memset(zeros_bf, 0.0), writes=[zerosB])
            P.op("dve", lambda e: e.tensor_scalar(out=negones[:], in0=ones_bf[:], scalar1=-1.0, scalar2=None, op0=ALU.mult),
                 reads=[CB["ones_bf"]], writes=[CB["negones"]])
            steps = []
            for hh in range(4):
                for i in range(4):
                    n = 4 * i + 4
                    for j, p in enumerate(range(n - 1, -1, -1)):
                        steps.append({"hs": (hh, hh + 4), "i": i, "n": n, "p": p, "j": j})
            zr = [kB, qB, CB["ident"], CB["negmask"]]

            def zmm(e, o_, h, i, p, first, last):
                msk = p >= 4 * i
                ins = e.matmul(o_, lhsT=k_fm[:, h, p * 128:(p + 1) * 128], rhs=q_fm[:, h, i * 256:(i + 1) * 256],
                               start=first, stop=(last and not msk))
                if msk:
                    ins = e.matmul(o_, lhsT=ident[:], rhs=negmask[:, p - 4 * i, :], start=False, stop=last)
                return ins

            def st1(m):
                s_ = steps[m]
                i, p = s_["i"], s_["p"]
                zb = m % 2

                def fn(e):
                    ins = None
                    for sl_, h in enumerate(s_["hs"]):
                        ins = zmm(e, ps[zb][:, sl_ * 256:(sl_ + 1) * 256], h, i, p, True, True)
                    return ins
                P.op("pe", fn, reads=zr, writes=[psB[zb]])

            def st2(m):
                zb = m % 2
                ee, eeB = e_r[m % 3]
                P.op("act", lambda e: e.activation(out=ee, in_=ps[zb][:, :], func=AF.Exp), reads=[psB[zb]], writes=[eeB])

            def st3(m):
                ee, eeB = e_r[m % 3]
                l_, lB_ = l_r[m % 3]
                P.op("act", lambda e: e.activation(out=l_, in_=ee, func=AF.Ln, bias=1.0), reads=[eeB], writes=[lB_])

            def st4(m):
                s_ = steps[m]
                i, j, p = s_["i"], s_["j"], s_["p"]
                l_, lB_ = l_r[m % 3]
                sbk = 2 + m % 3
                r_old, r_oldB = R32[j % 2]
                r_new, r_newB = R32[(j + 1) % 2]
                rb_old, rb_oldB = Rb[j % 2]
                rb_new, rb_newB = Rb[(j + 1) % 2]

                def fn(e):
                    ins = None
                    for sl_, h in enumerate(s_["hs"]):
                        o_ = ps[sbk][:, sl_ * 256:(sl_ + 1) * 256]
                        zmm(e, o_, h, i, p, True, False)
                        ins = e.matmul(o_, lhsT=negtri[:], rhs=l_[:, sl_ * 256:(sl_ + 1) * 256], start=False, stop=(j == 0))
                        if j > 0:
                            ins = e.matmul(o_, lhsT=negones[:], rhs=rb_old[:, sl_ * 256:(sl_ + 1) * 256], start=False, stop=True)
                    return ins
                rd = zr + [lB_, CB["negtri"]] + ([rb_oldB, CB["negones"]] if j > 0 else [])
                P.op("pe", fn, reads=rd, writes=[psB[sbk]])
                if p > 0:
                    if j == 0:
                        P.op("dve", lambda e: e.tensor_tensor(out=r_new, in0=l_, in1=zeros_bf, op=ALU.add),
                             reads=[lB_, zerosB], writes=[r_newB])
                        P.op("pool", lambda e: e.tensor_tensor(out=rb_new, in0=l_, in1=zeros_bf, op=ALU.add),
                             reads=[lB_, zerosB], writes=[rb_newB])
                    else:
                        P.op("dve", lambda e: e.tensor_tensor(out=r_new, in0=r_old, in1=l_, op=ALU.add),
                             reads=[r_oldB, lB_], writes=[r_newB])
                        P.op("pool", lambda e: e.tensor_tensor(out=rb_new, in0=r_old, in1=l_, op=ALU.add),
                             reads=[r_oldB, lB_], writes=[rb_newB])

            def st5(m):
                a_, aB_ = a_r[m % 3]
                sbk = 2 + m % 3
                P.op("act", lambda e: e.activation(out=a_, in_=ps[sbk][:, :], func=AF.Exp), reads=[psB[sbk]], writes=[aB_])

            def st6(m):
                s_ = steps[m]
                i, p, n = s_["i"], s_["p"], s_["n"]
                a_, aB_ = a_r[m % 3]

                def fn(e):
                    ins = None
                    for sl_, h in enumerate(s_["hs"]):
                        ins = e.matmul(ps[6 + sl_][:, 0:256], lhsT=v_tm[:, p, h * 128:(h + 1) * 128],
                                       rhs=a_[:, sl_ * 256:(sl_ + 1) * 256], start=(p == n - 1), stop=(p == 0))
                    return ins
                P.op("pe", fn, reads=[vB, aB_], writes=[psB[6], psB[7]])
                if p == 0:
                    for sl_, h in enumerate(s_["hs"]):
                        P.op("dve", lambda e, sl_=sl_, h=h: e.tensor_copy(out=o_fm[:, h, i * 256:(i + 1) * 256],
                                                                          in_=ps[6 + sl_][:, 0:256]),
                             reads=[psB[6 + sl_]], pwrites=[oB])

            stages = [st1, st2, st3, st4, st5, st6]
            NS = len(steps)
            ada_late = list(ADA_LATE)
            for k in range(NS + len(stages) - 1):
                for d_, fn_ in enumerate(stages):
                    if 0 <= k - d_ < NS:
                        fn_(k - d_)
                if k % 10 == 4 and ada_late:
                    ada_cols(ada_late.pop(0), bank=5)
            while ada_late:
                ada_cols(ada_late.pop(0), bank=5)

            P.barrier()
            if STAGE == 4:
                raise _Stop()
            h_own3 = av(32, 36).rearrange("p (k t) -> p k t", k=16)
            hown3Bs = [Buf("h_own3_t0"), Buf("h_own3_t1")]
            xtb3 = []
            for i in range(2):
                k = P.new_sem("ld_xu%d" % i)
                xtb3.append((av(68 + 8 * i, 8, F32), Buf("xu%d" % i), k))
            hns3 = [(av(84, 4), Buf("hn3a")), (av(92, 4), Buf("hn3b"))]
            merged = av(112, 32).rearrange("p (c t) -> p c t", c=16)
            mergedB = Buf("merged")
            sg = av(144, 8).rearrange("p (c t) -> p c t", c=4)
            sgB = Buf("sg")
            t1 = av(0, 16, F32).rearrange("p (c t) -> p c t", c=4)
            t1B = Buf("t1")
            t2s = [av(88, 2, F32), av(90, 2, F32)]
            t2B = [Buf("t2a"), Buf("t2b")]
            norm_seq([(xo[tb * 128:(tb + 1) * 128, :], None, xtb3, hns3,
                       h_own3[:, :, tb * 128:(tb + 1) * 128], hown3Bs[tb // 4], A1, B1, modB, True) for tb in range(8)])
            tc = [0]
            for s in range(4):
                slga, sBga, slotga = wnext()
                for th in range(2):
                    for c in range(4):
                        b = nbank()
                        mm_group(ps[b][:, :], [(slga[:, k, c * 128:(c + 1) * 128], h_own3[:, k, th * 512:(th + 1) * 512])
                                               for k in range(16)], reads=sBga + [hown3Bs[th]], writes=[psB[b]])
                        P.op("act", lambda e, b=b, c=c, th=th: e.activation(out=sg[:, c, th * 512:(th + 1) * 512], in_=ps[b][:, :],
                                                                           func=AF.Sigmoid), reads=[psB[b]], writes=[sgB])
                wrelease(slotga)
                slab_, sBab, slotab = wnext()
                for c in range(4):
                    for th in range(2):
                        b = nbank()
                        mm_group(ps[b][:, :], [(slab_[:, k, c * 128:(c + 1) * 128], mixed[:, k, th * 512:(th + 1) * 512])
                                               for k in range(8)], reads=[sBab[0], mixB], writes=[psB[b]])
                        P.op("dve", lambda e, b=b, c=c, th=th: e.tensor_tensor(out=t1[:, c, th * 512:(th + 1) * 512], in0=ps[b][:, :],
                                                                              in1=sg[:, c, th * 512:(th + 1) * 512], op=ALU.mult),
                             reads=[psB[b], sgB], writes=[t1B])
                slgb, sBgb, slotgb = wnext()
                for c in range(4):
                    for th in range(2):
                        b = nbank()
                        mm_group(ps[b][:, :], [(slgb[:, k, c * 128:(c + 1) * 128], h_own3[:, k, th * 512:(th + 1) * 512])
                                               for k in range(16)], reads=sBgb + [hown3Bs[th]], writes=[psB[b]])
                        P.op("act", lambda e, b=b, c=c, th=th: e.activation(out=sg[:, c, th * 512:(th + 1) * 512], in_=ps[b][:, :],
                                                                           func=AF.Sigmoid), reads=[psB[b]], writes=[sgB])
                wrelease(slotgb)
                for c in range(4):
                    for th in range(2):
                        b = nbank()
                        mm_group(ps[b][:, :], [(slab_[:, 8 + k, c * 128:(c + 1) * 128], o_fm[:, k, th * 512:(th + 1) * 512])
                                               for k in range(8)], reads=[sBab[1], oB], writes=[psB[b]])
                        ti = tc[0] % 2
                        tc[0] += 1
                        P.op("dve", lambda e, b=b, c=c, th=th, ti=ti: e.tensor_tensor(out=t2s[ti], in0=ps[b][:, :],
                                                                                     in1=sg[:, c, th * 512:(th + 1) * 512], op=ALU.mult),
                             reads=[psB[b], sgB], writes=[t2B[ti]])
                        P.op("pool", lambda e, c=c, th=th, ti=ti, s=s: e.tensor_tensor(out=merged[:, 4 * s + c, th * 512:(th + 1) * 512],
                                                                                      in0=t1[:, c, th * 512:(th + 1) * 512], in1=t2s[ti],
                                                                                      op=ALU.add), reads=[t1B, t2B[ti]], writes=[mergedB])
                wrelease(slotab)

            P.barrier()
            if STAGE == 5:
                raise _Stop()
            gate_bc = [av(96, 8, F32), av(104, 8, F32)]
            gateB = [Buf("g1"), Buf("g2")]
            A2B = Buf("A2")
            P.op("dve", lambda e: e.scalar_tensor_tensor(out=A2[:], in0=modfm[:, 64:80], scalar=1.0, in1=n2w_t[:],
                                                         op0=ALU.add, op1=ALU.mult), reads=[modB, CB["n2w"]], writes=[A2B])
            gB = Buf("gtmp")
            gTB = Buf("gT")
            for gi, c0 in enumerate([32, 80]):
                P.op("dve", lambda e, c0=c0: e.tensor_copy(out=gh_bf[:], in_=modfm[:, c0:c0 + 16]), reads=[modB], writes=[gB])
                P.op("dve", lambda e: e.tensor_copy(out=gh32[:], in_=gh_bf[:]), reads=[gB], writes=[gB])
                P.op("dve", lambda e, c0=c0: e.tensor_tensor(out=gl32[:], in0=modfm[:, c0:c0 + 16], in1=gh32[:], op=ALU.subtract),
                     reads=[modB, gB], writes=[gB])
                P.op("dve", lambda e: e.tensor_copy(out=gl_bf[:], in_=gl32[:]), reads=[gB], writes=[gB])
                bt = nbank()
                tv = ps[bt][:].bitcast(BF16)

                def trg(e, tv=tv):
                    e.transpose(tv[0:16, 0:128], gh_bf[:], ident[:])
                    return e.transpose(tv[0:16, 128:256], gl_bf[:], ident[:])
                P.op("pe", trg, reads=[gB, CB["ident"]], writes=[psB[bt]])
                P.op("dve", lambda e, tv=tv: e.tensor_copy(out=gT[:], in_=tv[0:16, 0:256]), reads=[psB[bt]], writes=[gTB])
                for c4 in range(4):
                    b = nbank()

                    def bc(e, b=b, c4=c4):
                        ins = None
                        for cc in range(4):
                            c = c4 * 4 + cc
                            o_ = ps[b][:, cc * 128:(cc + 1) * 128]
                            e.matmul(o_, lhsT=St[:, c, :], rhs=gT[:, 0:128], start=True, stop=False)
                            ins = e.matmul(o_, lhsT=St[:, c, :], rhs=gT[:, 128:256], start=False, stop=True)
                        return ins
                    P.op("pe", bc, reads=[gTB, CB["S"]], writes=[psB[b]])
                    P.op("act", lambda e, b=b, gi=gi, c4=c4: e.activation(out=gate_bc[gi][:, c4 * 512:(c4 + 1) * 512], in_=ps[b][:, :],
                                                                         func=AF.Identity), reads=[psB[b]], writes=[gateB[gi]])

            x1 = av(32, 64, F32).rearrange("p (b d) -> p b d", b=8)
            x1B = [Buf("x1_%d" % i) for i in range(8)]
            h2_fm = av(0, 32).rearrange("p (k t) -> p k t", k=16)
            h2Bs = [Buf("h2_t0"), Buf("h2_t1")]
            t4 = [av(144, 2, F32), av(146, 2, F32)]
            t4B = [Buf("t4a"), Buf("t4b")]
            hns4 = [(av(148, 4), Buf("hn4a")), (av(152, 4), Buf("hn4b"))]
            P.new_sem("ld_x1")
            for tb in range(8):
                P.op("sp", lambda e, tb=tb: e.dma_start(out=x1[:, tb, :], in_=xo[tb * 128:(tb + 1) * 128, :]),
                     writes=[x1B[tb]], sem="ld_x1", inc=16)
            fin = ("ld_x1", P.cnt["ld_x1"])
            for tb in range(8):
                x1B[tb].set_w(fin)

            def resid_epilogue(b, gi, s, tb, tbuf, tbufB):
                P.op("dve", lambda e: e.tensor_tensor(out=tbuf, in0=ps[b][:, :], in1=gate_bc[gi][:, s * 512:(s + 1) * 512], op=ALU.mult),
                     reads=[psB[b], gateB[gi]], writes=[tbufB])
                P.op("pool", lambda e: e.tensor_tensor(out=x1[:, tb, s * 512:(s + 1) * 512], in0=x1[:, tb, s * 512:(s + 1) * 512],
                                                       in1=tbuf, op=ALU.add), reads=[tbufB, x1B[tb]], writes=[x1B[tb]])

            np4 = NormPipe()
            for s in range(4):
                sl, sB, slot = wnext()
                for tb in range(8):
                    b = nbank()
                    mm_group(ps[b][:, :], [(merged[:, k, tb * 128:(tb + 1) * 128], sl[:, k, :]) for k in range(16)],
                             reads=sB + [mergedB], writes=[psB[b]])
                    ti = tc[0] % 2
                    tc[0] += 1
                    resid_epilogue(b, 0, s, tb, t4[ti], t4B[ti])
                    if s == 3:
                        np4.push((x1[:, tb, :], [x1B[tb]], None, hns4, h2_fm[:, :, tb * 128:(tb + 1) * 128], h2Bs[tb // 4],
                                  A2, B2, modB, False))
                wrelease(slot)
            np4.flush()

            a_q = av(112, 32).rearrange("p (f t) -> p f t", f=16)
            aqB = Buf("a_q")
            r5 = [av(96, 2, F32), av(98, 2, F32)]
            r5B = [Buf("r5a"), Buf("r5b")]
            t5 = [av(100, 2, F32), av(102, 2, F32)]
            t5B = [Buf("t5a"), Buf("t5b")]
            for q in range(4):
                for s in range(4):
                    sl, sB, slot = wnext()
                    for th in range(2):
                        for c in range(4):
                            b = nbank()
                            mm_group(ps[b][:, :], [(sl[:, k, c * 128:(c + 1) * 128], h2_fm[:, k, th * 512:(th + 1) * 512])
                                                   for k in range(16)], reads=sB + [h2Bs[th]], writes=[psB[b]])
                            ri = tc[0] % 2
                            tc[0] += 1
                            P.op("act", lambda e, b=b, ri=ri: e.activation(out=r5[ri], in_=ps[b][:, :], func=AF.Relu),
                                 reads=[psB[b]], writes=[r5B[ri]])
                            P.op("pool", lambda e, ri=ri, s=s, c=c, th=th: e.tensor_tensor(out=a_q[:, 4 * s + c, th * 512:(th + 1) * 512],
                                                                                          in0=r5[ri], in1=r5[ri], op=ALU.mult),
                                 reads=[r5B[ri]], writes=[aqB])
                    wrelease(slot)
                for s in range(4):
                    sl, sB, slot = wnext()
                    for tb in range(8):
                        b = nbank()
                        mm_group(ps[b][:, :], [(a_q[:, k, tb * 128:(tb + 1) * 128], sl[:, k, :]) for k in range(16)],
                                 reads=sB + [aqB], writes=[psB[b]])
                        ti = tc[0] % 2
                        tc[0] += 1
                        resid_epilogue(b, 1, s, tb, t5[ti], t5B[ti])
                    wrelease(slot)
        except _Stop:
            truncated = True
        P.new_sem("st_out")
        if truncated:
            x1 = av(32, 64, F32).rearrange("p (b d) -> p b d", b=8)
            x1B = [Buf("x1t_%d" % i) for i in range(8)]
            P.barrier()
            for tb in range(8):
                P.op("dve", lambda e, tb=tb: e.memset(x1[:, tb, :], 0.0), writes=[x1B[tb]])
        for tb in range(8):
            P.op("sp", lambda e, tb=tb: e.dma_start(out=out[tb * 128:(tb + 1) * 128, :], in_=x1[:, tb, :]),
                 reads=[x1B[tb]], sem="st_out", inc=16)
        P.wait_all("sp", ["st_out"] + [k for k in P.semnames if k.startswith("ld_")])
        if not truncated:
            assert wstate["next_use"] == len(sched), (wstate["next_use"], len(sched))
        print("op counts", {k: v for k, v in P.cnt.items()})
        P.build()
    return nc


def _consts():
    ident = np.eye(128, dtype=np.float32)
    j = np.arange(128)
    negtri = -(j[:, None] >= j[None, :]).astype(np.float32)
    ones = np.ones((128, 128), np.float32)
    E = np.zeros((128, 16, 16), np.float32)
    for p in range(16):
        E[:, p, p] = 1.0
    S = np.zeros((16, 16, 128), np.float32)
    for p in range(16):
        S[p, p, :] = 1.0
    return ident, negtri, ones, E.reshape(128, 256), S.reshape(16, 2048)


def _negmask(half):
    goff = [0, 3] if half == 0 else [1, 2]
    m = np.zeros((128, 4, 256), np.float32)
    s = np.arange(128)[:, None]
    t = np.arange(128)[None, :]
    tri = np.where(s < t, 0.0, NEG).astype(np.float32)
    for o in range(4):
        for qh in range(2):
            sl = slice(qh * 128, (qh + 1) * 128)
            if o < goff[qh]:
                m[:, o, sl] = 0.0
            elif o == goff[qh]:
                m[:, o, sl] = tri
            else:
                m[:, o, sl] = NEG
    return m.reshape(128, 1024)


_NC_CACHE = {}


def kernel(x, c, w_ada, b_ada, norm1_w, w_in, q_norm_w, k_norm_w, w_pool, pool_scale,
           w_a_up, w_b_up, w_o, norm2_w, w_ff1, w_ff2):
    f = lambda a: np.ascontiguousarray(np.asarray(a, dtype=np.float32))
    x = f(x); c = f(c)
    if "nc" not in _NC_CACHE:
        _NC_CACHE["nc"] = build_program()
    nc = _NC_CACHE["nc"]
    ident, negtri, ones, E, S = _consts()
    bada_fm = f(np.asarray(b_ada)[0].reshape(96, 128).T)
    n1 = f(np.asarray(norm1_w)[0].reshape(16, 128).T)
    n2 = f(np.asarray(norm2_w)[0].reshape(16, 128).T)
    qkw = f(np.stack([np.asarray(q_norm_w)[0], np.asarray(k_norm_w)[0]], axis=1))
    psc = f(np.asarray(pool_scale)[0].reshape(8, 128).T)
    shared = {
        "bada": bada_fm, "n1w": n1, "n2w": n2, "qkw": qkw, "pscale": psc,
        "c_ident": ident, "c_negtri": negtri, "c_ones": ones, "c_E": E, "c_S": S,
        "w_ada": f(np.asarray(w_ada)[0]), "w_in": f(np.asarray(w_in)[0]), "w_pool": f(np.asarray(w_pool)[0]),
        "w_a_up": f(np.asarray(w_a_up)[0]), "w_b_up": f(np.asarray(w_b_up)[0]), "w_o": f(np.asarray(w_o)[0]),
        "w_ff1": f(np.asarray(w_ff1)[0]), "w_ff2": f(np.asarray(w_ff2)[0]),
    }
    in_maps = []
    for r in range(8):
        b, half = r // 2, r % 2
        own = OWN[half]
        xb = x[b]
        xo = np.zeros((1152, 2048), np.float32)
        hv = np.zeros((128, 8, 16), np.float32)
        for j, g in enumerate(own):
            xo[j * 128:(j + 1) * 128] = xb[g * 128:(g + 1) * 128]
            if g > 0:
                xo[1024 + j * 16:1024 + (j + 1) * 16] = xb[g * 128 - 16:g * 128]
                hv[:, j, :] = 1.0
        invc = np.zeros((128, 4, 128), np.float32)
        for g, w in enumerate([2, 4, 8, 16]):
            if half == 0:
                cnt = np.minimum(np.arange(128) + 1, w).astype(np.float32)
            else:
                cnt = np.full(128, w, np.float32)
            invc[:, g, :] = (1.0 / cnt)[None, :]
        m = dict(shared)
        m.update({
            "xa": xb, "xo": xo, "csil": f(c[b].reshape(16, 128).T), "hv16": hv.reshape(128, 128),
            "invc": invc.reshape(128, 512), "c_negmask": _negmask(half),
        })
        in_maps.append(m)
    res = run_bass_kernel_spmd(nc, in_maps, core_ids=list(range(8)))
    outp = np.empty((4, 2048, 2048), np.float32)
    for r in range(8):
        b, half = r // 2, r % 2
        o = res.results[r]["out"]
        for j, g in enumerate(OWN[half]):
            outp[b, g * 128:(g + 1) * 128] = o[j * 128:(j + 1) * 128]
    return outp
```
